# Optimizing a Trainium2 kernel written in Bass

```python
import jax, jax.numpy as jnp
from jax import lax
import numpy as np

D_MODEL = 1024
BATCH = 4
SEQ = 8192
DEPTH = 1

POOL_WINDOWS = (2, 4, 8, 16)
POOL_WIDTH = D_MODEL // 2
POOL_GROUP = POOL_WIDTH // len(POOL_WINDOWS)
HEAD_DIM = 64
N_HEADS = (D_MODEL // 2) // HEAD_DIM
N_KV_HEADS = 2
ATTN_WIDTH = N_HEADS * HEAD_DIM
KV_WIDTH = N_KV_HEADS * HEAD_DIM
WINDOW = 128
BLOCK = 128
ROPE_THETA = 500000.0
ROT_DIM = HEAD_DIM // 4
MIX_WIDTH = POOL_WIDTH + ATTN_WIDTH
IN_WIDTH = POOL_WIDTH + ATTN_WIDTH + 2 * KV_WIDTH
D_FF = 2816
EPS = 1e-6
NEG_INF = -1e30

kernel_name = "hybrid_pool_swa_macaron_block"


def rms_norm(x, g):
    xf = x.astype(jnp.float32)
    y = xf * lax.rsqrt(jnp.mean(xf * xf, axis=-1, keepdims=True) + EPS)
    return (y * g.astype(jnp.float32)).astype(x.dtype)


def swiglu(h, w_gu, w_down):
    gate, up = jnp.split(h @ w_gu, 2, axis=-1)
    return (jax.nn.silu(gate) * up) @ w_down


def pool_mix(u, w_pool, pool_scale):
    B, S, C = u.shape
    uf = u.astype(jnp.float32)
    cs = jnp.concatenate([jnp.zeros((B, 1, C), jnp.float32), jnp.cumsum(uf, axis=1)], axis=1)
    t = jnp.arange(S)
    outs = []
    for g, w in enumerate(POOL_WINDOWS):
        lo, hi = g * POOL_GROUP, (g + 1) * POOL_GROUP
        start = jnp.maximum(t + 1 - w, 0)
        cnt = (t + 1 - start).astype(jnp.float32)
        csg = cs[:, :, lo:hi]
        mean = (csg[:, 1:] - csg[:, start]) / cnt[None, :, None]
        outs.append(mean - uf[:, :, lo:hi])
    d = jnp.stack(outs, axis=2).astype(u.dtype)
    y = jnp.einsum('bsgc,gcd->bsgd', d, w_pool).reshape(B, S, POOL_WIDTH)
    return y * pool_scale


def apply_partial_rope(x, cos, sin):
    half = ROT_DIM // 2
    x1, x2 = x[..., :half], x[..., half:ROT_DIM]
    rot = jnp.concatenate([x1 * cos - x2 * sin, x2 * cos + x1 * sin], axis=-1)
    return jnp.concatenate([rot, x[..., ROT_DIM:]], axis=-1)


def swa_with_sinks(q, k, v, sinks):
    B, S = q.shape[0], q.shape[1]
    nb = S // BLOCK
    G = N_HEADS // N_KV_HEADS
    qb = q.reshape(B, nb, BLOCK, N_KV_HEADS, G, HEAD_DIM)
    pad = ((0, 0), (BLOCK, 0), (0, 0), (0, 0))
    kp = jnp.pad(k, pad).reshape(B, nb + 1, BLOCK, N_KV_HEADS, HEAD_DIM)
    vp = jnp.pad(v, pad).reshape(B, nb + 1, BLOCK, N_KV_HEADS, HEAD_DIM)
    kb = jnp.concatenate([kp[:, :-1], kp[:, 1:]], axis=2)
    vb = jnp.concatenate([vp[:, :-1], vp[:, 1:]], axis=2)
    s = jnp.einsum('bnqkgd,bnjkd->bnkgqj', qb, kb,
                   preferred_element_type=jnp.float32) * (HEAD_DIM ** -0.5)
    qi = jnp.arange(BLOCK)[:, None]
    kj = jnp.arange(2 * BLOCK)[None, :]
    diff = qi + BLOCK - kj
    band = (diff >= 0) & (diff < WINDOW)
    key_abs = jnp.arange(nb)[:, None] * BLOCK - BLOCK + kj
    valid = band[None] & (key_abs >= 0)[:, None, :]
    s = jnp.where(valid[None, :, None, None], s, NEG_INF)
    sink = jnp.broadcast_to(sinks.astype(jnp.float32).reshape(1, 1, N_KV_HEADS, G, 1, 1),
                            s.shape[:-1] + (1,))
    p = jax.nn.softmax(jnp.concatenate([s, sink], axis=-1), axis=-1)[..., :-1]
    o = jnp.einsum('bnkgqj,bnjkd->bnqkgd', p.astype(v.dtype), vb)
    return o.reshape(B, S, ATTN_WIDTH)


def token_mixer(h, cos, sin, w_in, w_pool, pool_scale, sinks, g_pool, g_attn, w_out):
    B, S, _ = h.shape
    z = h @ w_in
    u = z[..., :POOL_WIDTH]
    q = z[..., POOL_WIDTH:MIX_WIDTH].reshape(B, S, N_HEADS, HEAD_DIM)
    k = z[..., MIX_WIDTH:MIX_WIDTH + KV_WIDTH].reshape(B, S, N_KV_HEADS, HEAD_DIM)
    v = z[..., MIX_WIDTH + KV_WIDTH:].reshape(B, S, N_KV_HEADS, HEAD_DIM)
    pool_out = pool_mix(u, w_pool, pool_scale)
    q = apply_partial_rope(q, cos, sin)
    k = apply_partial_rope(k, cos, sin)
    attn_out = swa_with_sinks(q, k, v, sinks)
    y = jnp.concatenate([rms_norm(pool_out, g_pool), rms_norm(attn_out, g_attn)], axis=-1)
    return y @ w_out


def setup_inputs(seed: int = 0) -> dict:
    key = jax.random.key(seed)
    ks = jax.random.split(key, 24)
    f32 = jnp.float32

    def w(k, shape, fan_in):
        return jax.random.normal(k, shape, f32) * fan_in ** -0.5

    def gain(k, n):
        return 1.0 + 0.05 * jax.random.normal(k, (DEPTH, n), f32)

    x = jax.random.normal(ks[0], (BATCH, SEQ, D_MODEL), f32)
    offset = jax.random.randint(ks[1], (BATCH, 1), 0, 4096, jnp.int32)
    positions = offset + jnp.arange(SEQ, dtype=jnp.int32)[None, :]
    return {
        "x": x,
        "positions": positions,
        "ffn1_pre": gain(ks[2], D_MODEL),
        "ffn1_w_gu": w(ks[3], (DEPTH, D_MODEL, 2 * D_FF), D_MODEL),
        "ffn1_w_down": w(ks[4], (DEPTH, D_FF, D_MODEL), D_FF),
        "ffn1_post": gain(ks[5], D_MODEL),
        "mix_pre": gain(ks[6], D_MODEL),
        "w_in": w(ks[7], (DEPTH, D_MODEL, IN_WIDTH), D_MODEL),
        "w_pool": w(ks[8], (DEPTH, len(POOL_WINDOWS), POOL_GROUP, POOL_GROUP), POOL_GROUP),
        "pool_scale": 0.5 + 0.05 * jax.random.normal(ks[9], (DEPTH, POOL_WIDTH), f32),
        "sinks": 0.5 * jax.random.normal(ks[10], (DEPTH, N_HEADS), f32),
        "g_pool": gain(ks[11], POOL_WIDTH),
        "g_attn": gain(ks[12], ATTN_WIDTH),
        "w_out": w(ks[13], (DEPTH, MIX_WIDTH, D_MODEL), MIX_WIDTH),
        "mix_post": gain(ks[14], D_MODEL),
        "ffn2_pre": gain(ks[15], D_MODEL),
        "ffn2_w_gu": w(ks[16], (DEPTH, D_MODEL, 2 * D_FF), D_MODEL),
        "ffn2_w_down": w(ks[17], (DEPTH, D_FF, D_MODEL), D_FF),
        "ffn2_post": gain(ks[18], D_MODEL),
    }


def reference(x, positions, ffn1_pre, ffn1_w_gu, ffn1_w_down, ffn1_post,
              mix_pre, w_in, w_pool, pool_scale, sinks, g_pool, g_attn, w_out, mix_post,
              ffn2_pre, ffn2_w_gu, ffn2_w_down, ffn2_post):
    inv_freq = ROPE_THETA ** (-jnp.arange(0, ROT_DIM, 2, dtype=jnp.float32) / ROT_DIM)
    ang = positions.astype(jnp.float32)[..., None] * inv_freq
    cos = jnp.cos(ang)[:, :, None, :].astype(x.dtype)
    sin = jnp.sin(ang)[:, :, None, :].astype(x.dtype)
    for l in range(DEPTH):
        h = swiglu(rms_norm(x, ffn1_pre[l]), ffn1_w_gu[l], ffn1_w_down[l])
        x = x + 0.5 * rms_norm(h, ffn1_post[l])
        h = token_mixer(rms_norm(x, mix_pre[l]), cos, sin, w_in[l], w_pool[l], pool_scale[l],
                        sinks[l], g_pool[l], g_attn[l], w_out[l])
        x = x + rms_norm(h, mix_post[l])
        h = swiglu(rms_norm(x, ffn2_pre[l]), ffn2_w_gu[l], ffn2_w_down[l])
        x = x + 0.5 * rms_norm(h, ffn2_post[l])
    return x
```

```python
import numpy as np
from contextlib import ExitStack
import concourse.bass as bass
import concourse.mybir as mybir
from concourse.bass_utils import run_bass_kernel_spmd

F32 = mybir.dt.float32
BF16 = mybir.dt.bfloat16
I32 = mybir.dt.int32
AF = mybir.ActivationFunctionType
ALU = mybir.AluOpType

D = 1024
DFF = 2816
NFC = 22
NKC = 8
NB = 4
T = 512
NT_FULL = 8
SEQ = 8192
NTOK = 4096
EPS = 1e-6
NEG = -30000.0
POOL_W = (2, 4, 8, 16)
NSLOT = NB + 1
RING = 3
TWO_PI = 6.283185307179586


class Sem:
    def __init__(self, h):
        self.h = h
        self.count = 0


class Res:
    __slots__ = ("name", "w", "r", "excl")

    def __init__(self, name, excl=False):
        self.name = name
        self.w = None
        self.r = {}
        self.excl = excl


class Eng:
    def __init__(self, eng, sem, is_pe=False):
        self.eng = eng
        self.sem = sem
        self.is_pe = is_pe
        self.waited = {}


class Sched:
    def __init__(self):
        self.engs = {}

    def deps(self, E, reads, writes):
        need = {}

        def add(s, v):
            if need.get(s, 0) < v:
                need[s] = v

        for b in reads:
            if b.w is not None:
                add(*b.w)
            if b.excl:
                for s, v in b.r.items():
                    if s is not E.sem:
                        add(s, v)
        for b in writes:
            if b.w is not None:
                add(*b.w)
            for s, v in b.r.items():
                add(s, v)
        for s, v in need.items():
            if s is E.sem and E.is_pe:
                continue
            if E.waited.get(s, 0) >= v:
                continue
            E.eng.wait_ge(s.h, v)
            E.waited[s] = v

    def _mark(self, tag, reads, writes):
        s, v = tag
        for b in writes:
            b.w = tag
            b.r = {}
        for b in reads:
            if b not in writes:
                if b.r.get(s, 0) < v:
                    b.r[s] = v

    def op(self, ename, fn, reads=(), writes=(), signal=True):
        E = self.engs[ename]
        self.deps(E, reads, writes)
        inst = fn(E.eng)
        if signal:
            E.sem.count += 1
            inst.then_inc(E.sem.h, 1)
            tag = (E.sem, E.sem.count)
        else:
            tag = (E.sem, E.sem.count + 1)
        self._mark(tag, reads, writes)
        return inst

    def dma(self, qname, fn, sem, reads=(), writes=()):
        E = self.engs[qname]
        self.deps(E, reads, writes)
        inst = fn(E.eng)
        sem.count += 16
        inst.then_inc(sem.h, 16)
        self._mark((sem, sem.count), reads, writes)
        return inst


class _Stop(Exception):
    pass


def build(NT=NT_FULL, stop=None):
    nc = bass.Bass("TRN2", target_bir_lowering=False)
    es = ExitStack()
    try:
        _body(nc, es, NT, stop)
    except _Stop:
        pass
    es.close()
    return nc


def _body(nc, es, NT, stop):
    def stop_at(name):
        if stop == name:
            raise _Stop()

    def din(name, shape, dt=F32):
        return nc.dram_tensor(name, shape, dt, kind="ExternalInput").ap()

    x_d = din("x", [128 + NTOK, D])
    pos_d = din("pos", [128, 33], I32)
    wgu_d = [din("wgu1", [D, 2 * DFF]), din("wgu2", [D, 2 * DFF])]
    wd_d = [din("wd1", [DFF, D]), din("wd2", [DFF, D])]
    win_d = din("win", [D, 1280])
    wout_d = din("wout", [D, D])
    wpool_d = din("wpool", [4, 128, 128])
    gpost_d = din("gpost", [128, 3, D])
    gpre_d = din("gpreT", [128, 3, NKC])
    pgs_d = din("pgs", [128, 3, 512])
    sinks_d = din("sinks", [128, 8])
    invf_d = din("invf", [128, 8])
    masks_d = din("masks", [128, 3, 512])
    bands_d = din("bands", [128, 4, 512])
    ident_d = din("ident", [128, 128])
    out_d = nc.dram_tensor("out", [NTOK, D], F32, kind="ExternalOutput").ap()

    def dscr(name, shape):
        return nc.dram_tensor(name, shape, BF16, kind="Internal").ap()

    wgu_b = [dscr("wgu1b", [D, 2 * DFF]), dscr("wgu2b", [D, 2 * DFF])]
    wd_b = [dscr("wd1b", [DFF, D]), dscr("wd2b", [DFF, D])]
    win_b = dscr("winb", [D, 1280])
    wout_b = dscr("woutb", [D, D])
    wpool_b = dscr("wpoolb", [4, 128, 128])

    if True:
        def sb(name, shape, dt):
            return es.enter_context(nc.sbuf_tensor("sb_" + name, shape, dt))

        def new_sem(name):
            return Sem(es.enter_context(nc.semaphore(name)))

        S = Sched()
        S.engs["pe"] = Eng(nc.tensor, new_sem("s_pe"), is_pe=True)
        S.engs["act"] = Eng(nc.scalar, new_sem("s_act"))
        S.engs["dve"] = Eng(nc.vector, new_sem("s_dve"))
        S.engs["sp"] = Eng(nc.sync, None)
        S.engs["pool"] = Eng(nc.gpsimd, new_sem("s_pool"))

        xbuf = [sb(f"xbuf{i}", [128, NB, D], F32) for i in range(2)]
        xres = [[Res(f"x{i}_{b}") for b in range(NB)] for i in range(2)]
        hb = sb("hb", [128, D], BF16)
        r_hb = Res("hb")
        hT = sb("hT", [128, NKC, T], BF16)
        r_hT = [Res(f"hT{b}") for b in range(NB)]
        AT = sb("AT", [128, NFC, T], BF16)
        r_AT = [Res(f"AT{c}") for c in range(NFC)]
        ring = [sb(f"ring{i}", [128, NKC, 512], BF16) for i in range(RING)]
        r_ring = [Res(f"ring{i}") for i in range(RING)]
        ring_sem = [new_sem(f"s_ring{i}") for i in range(RING)]
        wd = sb("wd", [128, NFC, D], BF16)
        r_wd = Res("wd")
        wd_sem = new_sem("s_wd")
        gpost = sb("gpost", [128, 3, D], F32)
        gpreT = sb("gpreT", [128, 3, NKC], F32)
        pgs = sb("pgs", [128, 3, 512], F32)
        e_sb = [sb(f"e_sb{i}", [128, T], F32) for i in range(2)]
        r_e = [Res(f"e{i}") for i in range(2)]
        tpost = sb("tpost", [128, D], F32)
        r_tpost = Res("tpost")
        junk = sb("junk", [128, D], BF16)
        r_junk = Res("junk")
        r_junk2 = [Res("junk_a"), Res("junk_b")]
        stat = sb("stat", [128, 64], F32)
        u_tm = sb("u_tm", [128, NSLOT, 512], BF16)
        r_u = [Res(f"u{i}") for i in range(NSLOT)]
        qk_tm = [sb(f"qk_tm{i}", [128, 10, 64], BF16) for i in range(2)]
        r_qk = [Res(f"qk{i}") for i in range(2)]
        QT = sb("QT", [128, NB, 512], BF16)
        r_QT = [Res(f"QT{b}") for b in range(NB)]
        KT = sb("KT", [128, NSLOT * 128], BF16)
        r_KT = [Res(f"KT{i}") for i in range(NSLOT)]
        Vx = sb("Vx", [128, NSLOT, 2, 65], BF16)
        r_V = [Res(f"V{i}") for i in range(NSLOT)]
        PT = [sb(f"PT{g}", [128, 2, 512], BF16) for g in range(2)]
        r_PT = [Res(f"PT{g}") for g in range(2)]
        dT_sb = sb("dT_sb", [128, 4, 128], BF16)
        r_dT = Res("dT")
        t1 = sb("t1", [128, 512], F32)
        r_t1 = Res("t1")
        a_sb = sb("a_sb", [128, 512], F32)
        r_a = Res("a")
        ycat = [sb(f"ycat{i}", [128, D], BF16) for i in range(2)]
        r_ycat = [Res(f"ycat{i}") for i in range(2)]
        den_sb = sb("den_sb", [128, 8], F32)
        r_den = Res("den")
        rope_t = sb("rope_t", [128, 4, 10, 8], F32)
        r_rope = [[Res(f"rope{i}k"), Res(f"rope{i}q")] for i in range(4)]
        masks = sb("masks_b", [128, 3, 512], BF16)
        bands = sb("bands_b", [128, 4, 512], BF16)
        ident_f = sb("ident_f", [128, 128], F32)
        ident = sb("ident_b", [128, 128], BF16)
        wpool = sb("wpool", [128, 4, 128], BF16)
        sinks_sb = sb("sinks_sb", [128, 8], F32)
        esink = sb("esink", [128, 8], F32)
        invf = sb("invf", [128, 8], F32)
        pos_i = sb("pos_i", [128, 33], I32)
        pos_f = sb("pos_f", [128, 33], F32)
        ang = sb("ang", [128, 33, 8], F32)
        ang2 = sb("ang2", [128, 33, 8], F32)
        kf = sb("kf", [128, 33, 8], F32)
        ki = sb("ki", [128, 33, 8], I32)
        cos_t = sb("cos_t", [128, 33, 8], F32)
        sin_t = sb("sin_t", [128, 33, 8], F32)
        r_const = Res("const")
        ps = es.enter_context(nc.psum_tensor("ps", [128, 8 * 512], F32))
        r_bank = [Res(f"bank{i}", excl=True) for i in range(8)]

        def bank(i, n=1):
            return ps[:, i * 512:(i + n) * 512]

        def stat_col(i):
            return stat[:, i:i + 1]

        conv = {}

        def cast_rows(name, dst, src, nrows, step):
            s_ = new_sem("s_cv_" + name)
            r = Res("cv_" + name)
            for r0 in range(0, nrows, step):
                r1 = min(nrows, r0 + step)
                S.dma("pool", lambda e, r0=r0, r1=r1: e.dma_start(out=dst[r0:r1, :], in_=src[r0:r1, :]), s_, writes=[r])
            conv[name] = r

        def cast_wgu(f):
            cast_rows(f"gu{f}", wgu_b[f], wgu_d[f], D, 128)
            for i in range(4):
                conv[f"gu{f}_{i}"] = conv[f"gu{f}"]

        cast_rows("wpool", wpool_b.rearrange("g c d -> (g c) d"), wpool_d.rearrange("g c d -> (g c) d"), 512, 512)
        cast_wgu(0)
        cast_rows("wd0", wd_b[0], wd_d[0], DFF, 704)
        cast_rows("win", win_b, win_d, D, 512)
        cast_rows("wout", wout_b, wout_d, D, 512)
        cast_wgu(1)
        cast_rows("wd1", wd_b[1], wd_d[1], DFF, 704)

        csem = new_sem("s_const")
        for dst, src in ((gpost[:], gpost_d), (gpreT[:], gpre_d), (pgs[:], pgs_d), (sinks_sb[:], sinks_d),
                         (invf[:], invf_d), (ident_f[:], ident_d), (pos_i[:], pos_d)):
            S.dma("sp", lambda e, dst=dst, src=src: e.dma_start(out=dst, in_=src), csem, writes=[r_const])
        stage_m = [(xbuf[1][:, i, 0:512], xres[1][i]) for i in range(3)]
        stage_b = [(xbuf[1][:, i, 512:1024], xres[1][i]) for i in range(4)]
        for i, (dst, rs_) in enumerate(stage_m):
            S.dma("sp", lambda e, dst=dst, i=i: e.dma_start(out=dst, in_=masks_d[:, i, :]), csem, writes=[r_const, rs_])
        for i, (dst, rs_) in enumerate(stage_b):
            S.dma("sp", lambda e, dst=dst, i=i: e.dma_start(out=dst, in_=bands_d[:, i, :]), csem, writes=[r_const, rs_])
        wpsem = new_sem("s_wpool")
        r_wpool = Res("wpool")
        S.dma("sp", lambda e: e.dma_start(out=wpool[:], in_=wpool_b.rearrange("g c d -> c g d")), wpsem,
              reads=[conv["wpool"]], writes=[r_wpool])

        r_c2 = Res("const2")
        for i, (src, rs_) in enumerate(stage_m):
            S.op("dve", lambda e, i=i, src=src: e.tensor_copy(out=masks[:, i, :], in_=src), reads=[r_const, rs_],
                 writes=[r_c2])
        for i, (src, rs_) in enumerate(stage_b):
            S.op("dve", lambda e, i=i, src=src: e.tensor_copy(out=bands[:, i, :], in_=src), reads=[r_const, rs_],
                 writes=[r_c2])
        S.op("dve", lambda e: e.tensor_copy(out=ident[:], in_=ident_f[:]), reads=[r_const], writes=[r_c2])
        S.op("dve", lambda e: e.memset(Vx[:], 1.0), writes=r_V)
        S.op("dve", lambda e: e.tensor_scalar(out=gpreT[:], in0=gpreT[:], scalar1=32.0, scalar2=None, op0=ALU.mult),
             reads=[r_const], writes=[r_c2])
        for i, c in enumerate((16.0, 32.0, 16.0)):
            S.op("dve", lambda e, i=i, c=c: e.tensor_scalar(out=gpost[:, i, :], in0=gpost[:, i, :], scalar1=c,
                                                            scalar2=None, op0=ALU.mult), reads=[r_const], writes=[r_c2])
        s512 = float(np.sqrt(512.0))
        for i in (1, 2):
            S.op("dve", lambda e, i=i: e.tensor_scalar(out=pgs[:, i, :], in0=pgs[:, i, :], scalar1=s512,
                                                       scalar2=None, op0=ALU.mult), reads=[r_const], writes=[r_c2])
        r_ang = Res("ang")
        S.op("dve", lambda e: e.tensor_copy(out=pos_f[:], in_=pos_i[:]), reads=[r_const], writes=[r_ang])
        for f in range(8):
            S.op("dve", lambda e, f=f: e.tensor_scalar(out=ang[:, :, f], in0=pos_f[:], scalar1=invf[:, f:f + 1],
                                                      scalar2=None, op0=ALU.mult), reads=[r_const, r_ang], writes=[r_ang])

        def reduce_sin(src, dst):
            S.op("dve", lambda e: e.tensor_scalar(out=kf[:], in0=src[:], scalar1=1.0 / TWO_PI, scalar2=None,
                                                  op0=ALU.mult), reads=[r_ang], writes=[r_ang])
            S.op("dve", lambda e: e.tensor_copy(out=ki[:], in_=kf[:]), reads=[r_ang], writes=[r_ang])
            S.op("dve", lambda e: e.tensor_copy(out=kf[:], in_=ki[:]), reads=[r_ang], writes=[r_ang])
            S.op("dve", lambda e: e.scalar_tensor_tensor(out=src[:], in0=kf[:], scalar=-6.28125, in1=src[:],
                                                         op0=ALU.mult, op1=ALU.add), reads=[r_ang], writes=[r_ang])
            S.op("dve", lambda e: e.scalar_tensor_tensor(out=src[:], in0=kf[:], scalar=-0.0019353071795864769,
                                                         in1=src[:], op0=ALU.mult, op1=ALU.add),
                 reads=[r_ang], writes=[r_ang])
            S.op("dve", lambda e: e.tensor_scalar(out=kf[:], in0=src[:], scalar1=3.141592653589793,
                                                  scalar2=-TWO_PI, op0=ALU.is_gt, op1=ALU.mult),
                 reads=[r_ang], writes=[r_ang])
            S.op("dve", lambda e: e.tensor_tensor(out=src[:], in0=src[:], in1=kf[:], op=ALU.add),
                 reads=[r_ang], writes=[r_ang])
            S.op("dve", lambda e: e.tensor_scalar(out=kf[:], in0=src[:], scalar1=-3.141592653589793,
                                                  scalar2=TWO_PI, op0=ALU.is_lt, op1=ALU.mult),
                 reads=[r_ang], writes=[r_ang])
            S.op("dve", lambda e: e.tensor_tensor(out=src[:], in0=src[:], in1=kf[:], op=ALU.add),
                 reads=[r_ang], writes=[r_ang])
            S.op("act", lambda e: e.activation(out=dst[:], in_=src[:], func=AF.Sin), reads=[r_ang], writes=[r_c2])

        S.op("dve", lambda e: e.tensor_scalar(out=ang2[:], in0=ang[:], scalar1=1.5707963267948966, scalar2=None,
                                              op0=ALU.add), reads=[r_ang], writes=[r_ang])
        reduce_sin(ang, sin_t)
        reduce_sin(ang2, cos_t)
        S.op("act", lambda e: e.activation(out=esink[:], in_=sinks_sb[:], func=AF.Exp), reads=[r_const], writes=[r_c2])

        stop_at("consts")
        ring_items = []
        ring_state = {"loaded": 0, "consumed": 0}
        item_done = []

        def wgu_item(f, j):
            sg = j // 3

            def mk(gu):
                src = wgu_b[f].rearrange("(kc p) c -> p kc c", p=128)[:, :, gu * DFF + j * 256:gu * DFF + (j + 1) * 256]
                return lambda e, slot: e.dma_start(out=ring[slot][:, :, gu * 256:(gu + 1) * 256], in_=src)
            return ([mk(0), mk(1)], [conv[f"gu{f}_{sg}"]])

        def wmat_item(wb, lo, hi, cname):
            src = wb.rearrange("(kc p) c -> p kc c", p=128)[:, :, lo:hi]
            return ([lambda e, slot: e.dma_start(out=ring[slot][:, :, 0:hi - lo], in_=src)], [conv[cname]])

        def ring_prefetch():
            while ring_state["loaded"] < len(ring_items) and ring_state["loaded"] - ring_state["consumed"] < RING:
                k = ring_state["loaded"]
                slot = k % RING
                fns, reads = ring_items[k]
                for fn in fns:
                    S.dma("sp", lambda e, fn=fn, slot=slot: fn(e, slot), ring_sem[slot], reads=reads,
                          writes=[r_ring[slot]])
                ring_state["loaded"] += 1

        ring_pos = {"next": 0}

        def ring_take(n=1):
            k = ring_pos["next"]
            ring_pos["next"] += n
            assert ring_state["loaded"] >= k + n, "ring underflow"
            return [(k + i) % RING for i in range(n)]

        def ring_release(n=1):
            ring_state["consumed"] += n
            ring_prefetch()

        wd_state = {"content": None}

        WD_PIECES = [(0, 3), (3, 6), (6, 9), (9, 12), (12, 15), (15, 18), (18, 20), (20, 22)]
        deferred = []

        def flush_deferred(n=None):
            while deferred and (n is None or n > 0):
                deferred.pop(0)()
                if n is not None:
                    n -= 1

        def wd_piece(f, lo, hi):
            src = wd_b[f].rearrange("(fc p) d -> p fc d", p=128)[:, lo:hi, :]
            S.dma("pool", lambda e: e.dma_start(out=wd[:, lo:hi, :], in_=src), wd_sem,
                  reads=[conv[f"wd{f}"]], writes=[r_wd])

        def load_wd(f, spread=False):
            if wd_state["content"] == f:
                return
            wd_state["content"] = f
            for lo, hi in WD_PIECES:
                if spread:
                    deferred.append(lambda f=f, lo=lo, hi=hi: wd_piece(f, lo, hi))
                else:
                    wd_piece(f, lo, hi)

        stat_i = {"n": 0}

        def new_stat():
            i = stat_i["n"] % 64
            stat_i["n"] += 1
            return i, Res(f"stat{stat_i['n']}")

        def rstd_from(src_ap, src_res, n_eps, width):
            pieces = src_ap if isinstance(src_ap, list) else [src_ap]
            cols = []
            for pi_, pap in enumerate(pieces):
                i, r = new_stat()
                col = stat_col(i)
                w_ = width // len(pieces)
                S.op("act", lambda e, pap=pap, col=col, w_=w_, pi_=pi_: e.activation(
                    out=junk[:, pi_ * 512:pi_ * 512 + w_], in_=pap, func=AF.Square, accum_out=col),
                    reads=src_res, writes=[r, r_junk2[pi_]])
                cols.append((col, r))
            col, r = cols[0]
            if len(cols) == 2:
                i, r_s = new_stat()
                cs_ = stat_col(i)
                S.op("act", lambda e: e.activation(out=cs_, in_=cols[0][0], func=AF.Identity, bias=cols[1][0]),
                     reads=[cols[0][1], cols[1][1]], writes=[r_s])
                col, r = cs_, r_s
            i2, r2 = new_stat()
            col2 = stat_col(i2)
            S.op("act", lambda e: e.activation(out=col2, in_=col, func=AF.Ln, bias=eps_col[n_eps]),
                 reads=[r, r_c2], writes=[r2])
            i3, r3 = new_stat()
            col3 = stat_col(i3)
            S.op("act", lambda e: e.activation(out=col3, in_=col2, func=AF.Exp, scale=-0.5), reads=[r2], writes=[r3])
            return col3, r3

        epst = sb("epst", [128, 4], F32)
        S.op("dve", lambda e: e.memset(epst[:, 0:1], 1024 * EPS), writes=[r_c2])
        S.op("dve", lambda e: e.memset(epst[:, 1:2], 512 * EPS), writes=[r_c2])
        S.op("dve", lambda e: e.memset(epst[:, 2:3], 1.0), writes=[r_c2])
        eps_col = {1024: epst[:, 0:1], 512: epst[:, 1:2]}
        one_col = epst[:, 2:3]

        tp_bf = bank(0).bitcast(BF16)

        def prenorm(xb, xr, b, gi):
            rs, rsr = rstd_from(xb, [xr], 1024, D)
            S.op("dve", lambda e: e.tensor_scalar(out=hb[:], in0=xb, scalar1=rs, scalar2=None, op0=ALU.mult),
                 reads=[xr, rsr], writes=[r_hb])
            for kc in range(NKC):
                S.op("pe", lambda e, kc=kc: e.transpose(out=tp_bf[:, kc * 128:(kc + 1) * 128],
                                                        in_=hb[:, kc * 128:(kc + 1) * 128], identity=ident[:]),
                     reads=[r_hb, r_c2], writes=[r_bank[0]], signal=(kc == NKC - 1))
            gb = gpreT[:, gi, :].unsqueeze(2).to_broadcast([128, NKC, 128])
            S.op("dve", lambda e: e.tensor_tensor(out=hT[:, :, b * 128:(b + 1) * 128],
                                                  in0=tp_bf.rearrange("p (k t) -> p k t", k=NKC), in1=gb, op=ALU.mult),
                 reads=[r_bank[0], r_c2], writes=[r_hT[b]])

        def postnorm(ybanks, xb, xr, gi):
            yres = [r_bank[ybanks], r_bank[ybanks + 1]]
            rs, rsr = rstd_from([bank(ybanks), bank(ybanks + 1)], yres, 1024, D)
            for h in range(2):
                S.op("dve", lambda e, h=h: e.scalar_tensor_tensor(
                    out=tpost[:, h * 512:(h + 1) * 512], in0=bank(ybanks + h), scalar=rs,
                    in1=gpost[:, gi, h * 512:(h + 1) * 512], op0=ALU.mult, op1=ALU.mult),
                    reads=[yres[h], rsr, r_c2], writes=[r_tpost])
            S.op("dve", lambda e: e.tensor_tensor(out=xb, in0=tpost[:], in1=xb, op=ALU.add),
                 reads=[r_tpost], writes=[xr])

        def ffn_items(f):
            return [wgu_item(f, j) for j in range(11)]

        def ffn(f, xt, xrs, nb, gpre_i, gpost_i, pre_done=False, after_block=None, final_hook=True,
                hook_independent=False):
            N = nb * 128
            if not pre_done:
                for b in range(nb):
                    prenorm(xt[:, b, :], xrs[b], b, gpre_i)
            stop_at("prenorm")
            pp = 0
            for j in range(11):
                if j >= 1:
                    flush_deferred(2)
                slot = ring_take()[0]
                for s in range(2):
                    c = 2 * j + s
                    gb_, ub_ = 2 * pp, 2 * pp + 1
                    pp ^= 1
                    for which, bk in ((0, gb_), (1, ub_)):
                        for kc in range(NKC):
                            S.op("pe", lambda e, kc=kc, which=which, bk=bk: e.matmul(
                                bank(bk)[:, 0:N], lhsT=ring[slot][:, kc, which * 256 + s * 128:which * 256 + (s + 1) * 128],
                                rhs=hT[:, kc, 0:N], start=(kc == 0), stop=(kc == NKC - 1)),
                                reads=[r_ring[slot]] + r_hT[:nb], writes=[r_bank[bk]], signal=(kc == NKC - 1))
                    ei = c % 2
                    S.op("act", lambda e, ei=ei, gb_=gb_: e.activation(out=e_sb[ei][:, 0:N], in_=bank(gb_)[:, 0:N],
                                                                      func=AF.Silu),
                         reads=[r_bank[gb_]], writes=[r_e[ei]])
                    S.op("dve", lambda e, ub_=ub_, c=c, ei=ei: e.tensor_tensor(out=AT[:, c, 0:N], in0=bank(ub_)[:, 0:N],
                                                                              in1=e_sb[ei][:, 0:N], op=ALU.mult),
                         reads=[r_bank[ub_], r_e[ei]], writes=[r_AT[c]])
                ring_release()
            stop_at("up")
            flush_deferred()
            for b in range(nb):
                yb = 4 + 2 * (b % 2)
                for half in range(2):
                    for fc in range(NFC):
                        S.op("pe", lambda e, fc=fc, half=half, yb=yb, b=b: e.matmul(
                            bank(yb + half), lhsT=AT[:, fc, b * 128:(b + 1) * 128],
                            rhs=wd[:, fc, half * 512:(half + 1) * 512], start=(fc == 0), stop=(fc == NFC - 1)),
                            reads=r_AT + [r_wd], writes=[r_bank[yb + half]], signal=(fc == NFC - 1))
                if after_block is not None and b >= 1:
                    after_block(b - 1)
                if after_block is not None and hook_independent and final_hook and b == nb - 1:
                    after_block(nb - 1)
                postnorm(yb, xt[:, b, :], xrs[b], gpost_i)
            if after_block is not None and final_hook and not hook_independent:
                after_block(nb - 1)

        class Mixer:
            def __init__(self, xt, xrs, base, halo, blk0, first_core_block):
                self.xt, self.xrs, self.base, self.halo, self.blk0, self.first = xt, xrs, base, halo, blk0, first_core_block

            def take_in(self):
                self.su, self.sq, self.skv = ring_take(3)

            def A(self, b):
                prenorm(self.xt[:, b, :], self.xrs[b], b, 1)

            def B1(self, b):
                halo = self.halo
                sl = (self.base + 1 + b) % NSLOT
                tok = slice(b * 128, (b + 1) * 128)
                groups = [(self.su, 1, 512), (self.skv, 3, 256)] + ([] if halo else [(self.sq, 2, 512)])
                for slot, bk, ncol in groups:
                    for kc in range(NKC):
                        S.op("pe", lambda e, kc=kc, slot=slot, bk=bk, ncol=ncol: e.matmul(
                            bank(bk)[:, 0:ncol], lhsT=hT[:, kc, tok], rhs=ring[slot][:, kc, 0:ncol],
                            start=(kc == 0), stop=(kc == NKC - 1)),
                            reads=[r_ring[slot], r_hT[b]], writes=[r_bank[bk]], signal=(kc == NKC - 1))
                S.op("act", lambda e: e.activation(out=u_tm[:, sl, :], in_=bank(1), func=AF.Identity),
                     reads=[r_bank[1]], writes=[r_u[sl]])
                S.op("act", lambda e: e.activation(out=Vx[:, sl, :, 0:64],
                                                   in_=bank(3)[:, 128:256].rearrange("p (g d) -> p g d", g=2),
                                                   func=AF.Identity), reads=[r_bank[3]], writes=[r_V[sl]])
                blk = self.blk0 + b
                qk_t = qk_tm[b % 2]
                r_qk_ = r_qk[b % 2]
                parts = [(bank(3)[:, 0:128].rearrange("p (h d) -> p h d", d=64), 2, r_bank[3], 8)]
                if not halo:
                    parts.append((bank(2).rearrange("p (h d) -> p h d", d=64), 8, r_bank[2], 0))
                for zz, nh, zr_, h0 in parts:
                    cb = cos_t[:, blk, :].unsqueeze(1).to_broadcast([128, nh, 8])
                    sn = sin_t[:, blk, :].unsqueeze(1).to_broadcast([128, nh, 8])
                    x1, x2 = zz[:, :, 0:8], zz[:, :, 8:16]
                    tmp = [rope_t[:, i, h0:h0 + nh, :] for i in range(4)]
                    rr_ = [r_rope[i][0 if nh == 2 else 1] for i in range(4)]
                    for i, (xa, tb_) in enumerate(((x1, cb), (x2, sn), (x2, cb), (x1, sn))):
                        S.op("dve", lambda e, i=i, xa=xa, tb_=tb_: e.tensor_tensor(out=tmp[i], in0=xa, in1=tb_, op=ALU.mult),
                             reads=[zr_, r_c2], writes=[rr_[i]])
                    if nh == 2:
                        qo, zi, tm = qk_t[:, 8:10, :], zz, tmp
                    else:
                        pq = lambda ap: ap.rearrange("p (g c) d -> p g c d", g=2)
                        qo, zi, tm = qk_t[:, 0:8, :].rearrange("p (c g) d -> p g c d", g=2), pq(zz), [pq(t_) for t_ in tmp]
                    S.op("dve", lambda e: e.tensor_tensor(out=qo[..., 0:8], in0=tm[0], in1=tm[1], op=ALU.subtract),
                         reads=[rr_[0], rr_[1]], writes=[r_qk_])
                    S.op("dve", lambda e: e.tensor_tensor(out=qo[..., 8:16], in0=tm[2], in1=tm[3], op=ALU.add),
                         reads=[rr_[2], rr_[3]], writes=[r_qk_])
                    S.op("act", lambda e: e.activation(out=qo[..., 16:64], in_=zi[..., 16:64], func=AF.Identity),
                         reads=[zr_], writes=[r_qk_])

            def B2(self, b):
                halo = self.halo
                sl = (self.base + 1 + b) % NSLOT
                qk_t = qk_tm[b % 2]
                r_qk_ = r_qk[b % 2]
                tpq = tp_bf
                if not halo:
                    for c in range(4):
                        S.op("pe", lambda e, c=c: e.transpose(
                            out=tpq[:, c * 128:(c + 1) * 128],
                            in_=qk_t[:, 2 * c:2 * c + 2, :].rearrange("p h d -> p (h d)"), identity=ident[:]),
                             reads=[r_qk_, r_c2], writes=[r_bank[0]], signal=False)
                S.op("pe", lambda e: e.transpose(out=tpq[:, 512:640], in_=qk_t[:, 8:10, :].rearrange("p h d -> p (h d)"),
                                                 identity=ident[:]), reads=[r_qk_, r_c2], writes=[r_bank[0]])
                if not halo:
                    S.op("act", lambda e: e.activation(out=QT[:, b, :], in_=tpq[:, 0:512], func=AF.Identity),
                         reads=[r_bank[0]], writes=[r_QT[b]])
                S.op("act", lambda e: e.activation(out=KT[:, sl * 128:(sl + 1) * 128], in_=tpq[:, 512:640],
                                                   func=AF.Identity), reads=[r_bank[0]], writes=[r_KT[sl]])

            def C1(self, b):
                sl = (self.base + 1 + b) % NSLOT
                slp = (self.base + b) % NSLOT
                tok = slice(b * 128, (b + 1) * 128)
                first = self.first and b == 0
                mprev = 2 if first else 1
                ycat_, r_ycat_ = ycat[b % 2], r_ycat[b % 2]
                for g in range(2):
                    pr = slice(g * 64, (g + 1) * 64)
                    for kb, (ksl, mi) in enumerate(((slp, mprev), (sl, 0))):
                        bk = 4 + 2 * g + kb
                        S.op("pe", lambda e, bk=bk, ksl=ksl, pr=pr: e.matmul(
                            bank(bk), lhsT=KT[pr, ksl * 128:(ksl + 1) * 128], rhs=QT[pr, b, :], start=True, stop=False),
                            reads=[r_KT[ksl], r_QT[b]], writes=[r_bank[bk]], signal=False)
                        S.op("pe", lambda e, bk=bk, mi=mi: e.matmul(bank(bk), lhsT=ident[:], rhs=masks[:, mi, :],
                                                                    start=False, stop=True),
                             reads=[r_c2], writes=[r_bank[bk]])
                        S.op("act", lambda e, bk=bk, g=g, kb=kb: e.activation(out=PT[g][:, kb, :], in_=bank(bk),
                                                                              func=AF.Exp, scale=0.125),
                             reads=[r_bank[bk]], writes=[r_PT[g]])
                bc, bp = (2, 3) if first else (0, 1)
                for g in range(4):
                    cs = slice(g * 128, (g + 1) * 128)
                    S.op("pe", lambda e, cs=cs: e.matmul(bank(5)[:, cs], lhsT=u_tm[:, sl, cs], rhs=bands[:, bc, cs],
                                                         start=True, stop=False),
                         reads=[r_u[sl], r_c2], writes=[r_bank[5]], signal=False)
                    S.op("pe", lambda e, cs=cs: e.matmul(bank(5)[:, cs], lhsT=u_tm[64:128, slp, cs],
                                                         rhs=bands[64:128, bp, cs], start=False, stop=True),
                         reads=[r_u[slp], r_c2], writes=[r_bank[5]], signal=(g == 3))
                S.op("act", lambda e: e.activation(out=dT_sb[:], in_=bank(5).rearrange("p (g t) -> p g t", g=4),
                                                   func=AF.Identity), reads=[r_bank[5]], writes=[r_dT])
                for g in range(2):
                    ob = 4 + 2 * g
                    for c in range(4):
                        for kb, ksl in enumerate((slp, sl)):
                            S.op("pe", lambda e, g=g, c=c, kb=kb, ksl=ksl, ob=ob: e.matmul(
                                bank(ob)[:, c * 65:(c + 1) * 65], lhsT=PT[g][:, kb, c * 128:(c + 1) * 128],
                                rhs=Vx[:, ksl, g, :], start=(kb == 0), stop=(kb == 1)),
                                reads=[r_PT[g], r_V[ksl]], writes=[r_bank[ob]], signal=(c == 3 and kb == 1))
                for g in range(4):
                    cs = slice(g * 128, (g + 1) * 128)
                    S.op("pe", lambda e, g=g, cs=cs: e.matmul(bank(7)[:, cs], lhsT=dT_sb[:, g, :], rhs=wpool[:, g, :],
                                                              start=True, stop=True),
                         reads=[r_dT, r_wpool], writes=[r_bank[7]], signal=(g == 3))
                den = den_sb[:]
                for g in range(2):
                    ovg = bank(4 + 2 * g)[:, 0:260].rearrange("p (c d) -> p c d", c=4)
                    S.op("dve", lambda e, g=g, ovg=ovg: e.tensor_tensor(out=den_sb[:, g * 4:(g + 1) * 4], in0=ovg[:, :, 64],
                                                                       in1=esink[:, g * 4:(g + 1) * 4], op=ALU.add),
                         reads=[r_bank[4 + 2 * g], r_c2], writes=[r_den])
                S.op("dve", lambda e: e.reciprocal(out=den, in_=den), reads=[r_den], writes=[r_den])
                for g in range(2):
                    ovg = bank(4 + 2 * g)[:, 0:260].rearrange("p (c d) -> p c d", c=4)
                    S.op("dve", lambda e, g=g, ovg=ovg: e.tensor_tensor(
                        out=a_sb[:, g * 256:(g + 1) * 256].rearrange("p (c d) -> p c d", c=4), in0=ovg[:, :, 0:64],
                        in1=den_sb[:, g * 4:(g + 1) * 4].unsqueeze(2).to_broadcast([128, 4, 64]), op=ALU.mult),
                        reads=[r_bank[4 + 2 * g], r_den], writes=[r_a])
                S.op("dve", lambda e: e.tensor_tensor(out=t1[:], in0=bank(7), in1=pgs[:, 0, :], op=ALU.mult),
                     reads=[r_bank[7], r_c2], writes=[r_t1])
                rs_a, rsr_a = rstd_from(a_sb[:], [r_a], 512, 512)
                rs_p, rsr_p = rstd_from(t1[:], [r_t1], 512, 512)
                S.op("dve", lambda e: e.scalar_tensor_tensor(out=ycat_[:, 512:1024], in0=a_sb[:], scalar=rs_a,
                                                             in1=pgs[:, 2, :], op0=ALU.mult, op1=ALU.mult),
                     reads=[r_a, rsr_a, r_c2], writes=[r_ycat_])
                S.op("dve", lambda e: e.scalar_tensor_tensor(out=ycat_[:, 0:512], in0=t1[:], scalar=rs_p,
                                                             in1=pgs[:, 1, :], op0=ALU.mult, op1=ALU.mult),
                     reads=[r_t1, rsr_p, r_c2], writes=[r_ycat_])

            def C2(self, b):
                tok = slice(b * 128, (b + 1) * 128)
                ycat_, r_ycat_ = ycat[b % 2], r_ycat[b % 2]
                for kc in range(NKC):
                    S.op("pe", lambda e, kc=kc: e.transpose(out=tp_bf[:, kc * 128:(kc + 1) * 128],
                                                            in_=ycat_[:, kc * 128:(kc + 1) * 128], identity=ident[:]),
                         reads=[r_ycat_, r_c2], writes=[r_bank[0]], signal=(kc == NKC - 1))
                S.op("act", lambda e: e.activation(out=hT[:, :, tok], in_=tp_bf.rearrange("p (k t) -> p k t", k=NKC),
                                                   func=AF.Identity), reads=[r_bank[0]], writes=[r_hT[b]])

            def take_out(self):
                self.wslots = ring_take(2)

            def D(self, b):
                tok = slice(b * 128, (b + 1) * 128)
                yb = 4 + 2 * (b % 2)
                for half in range(2):
                    for kc in range(NKC):
                        S.op("pe", lambda e, kc=kc, half=half, yb=yb: e.matmul(
                            bank(yb + half), lhsT=hT[:, kc, tok], rhs=ring[self.wslots[half]][:, kc, :],
                            start=(kc == 0), stop=(kc == NKC - 1)),
                            reads=[r_hT[b], r_ring[self.wslots[half]]], writes=[r_bank[yb + half]],
                            signal=(kc == NKC - 1))
                postnorm(yb, self.xt[:, b, :], self.xrs[b], 1)

        def win_items():
            return [wmat_item(win_b, 0, 512, "win"), wmat_item(win_b, 512, 1024, "win"),
                    wmat_item(win_b, 1024, 1280, "win")]

        for t in range(max(NT, 1)):
            ring_items.extend(ffn_items(0))
            if t == 0:
                ring_items.extend(ffn_items(0))
                ring_items.extend(win_items())
            if NT > 0:
                ring_items.extend(win_items())
                ring_items.extend([wmat_item(wout_b, 0, 512, "wout"), wmat_item(wout_b, 512, 1024, "wout")])
                ring_items.extend(ffn_items(1))

        xsem = [new_sem("s_x0"), new_sem("s_x1")]
        osem = [new_sem("s_o0"), new_sem("s_o1")]

        def load_x(par, row0, nb):
            src = x_d[row0:row0 + nb * 128, :].rearrange("(b p) d -> p b d", p=128)
            S.dma("sp", lambda e: e.dma_start(out=xbuf[par][:, 0:nb, :], in_=src), xsem[par], writes=xres[par][:nb])

        load_x(0, 0, 1)
        if NT > 0:
            load_x(1, 128, NB)
        ring_prefetch()
        load_wd(0)

        def halo_pass():
            ffn(0, xbuf[0], xres[0], 1, 0, 0)
            mh = Mixer(xbuf[0], xres[0], NSLOT - 1, True, 0, False)
            mh.take_in()
            mh.A(0)
            mh.B1(0)
            mh.B2(0)
            ring_release(3)

        stop_at("loads")
        base = 0
        pre_done = False
        if NT == 0:
            ring_items[0:11] = []
            halo_pass()
        for t in range(NT):
            par = (t + 1) % 2
            xt, xrs = xbuf[par], xres[par]
            mx = Mixer(xt, xrs, base, False, 1 + t * NB, t == 0)
            if t == 0:
                ffn(0, xt, xrs, NB, 0, 0)
                halo_pass()
                for b in range(NB - 1):
                    mx.A(b)
            else:
                ffn(0, xt, xrs, NB, 0, 0, pre_done=pre_done, after_block=mx.A, final_hook=False)
            if t + 1 < NT:
                load_x(t % 2, 128 + (t + 1) * T, NB)
            load_wd(1)
            mx.take_in()
            for i in range(2 * NB + 1):
                if i < NB:
                    mx.B1(i)
                    if i == NB - 1:
                        ring_release(3)
                        mx.take_out()
                if 1 <= i <= NB:
                    mx.C1(i - 1)
                if i == 1:
                    mx.A(NB - 1)
                if 2 <= i <= NB + 1:
                    mx.C2(i - 2)
                if i < NB:
                    mx.B2(i)
                if NB <= i < 2 * NB:
                    mx.D(i - NB)
                if NB + 1 <= i:
                    prenorm(xt[:, i - NB - 1, :], xrs[i - NB - 1], i - NB - 1, 2)
            ring_release(2)
            base = (base + NB) % NSLOT
            nxt = None
            if t + 1 < NT:
                xn, xnr = xbuf[t % 2], xres[t % 2]
                nxt = lambda b, xn=xn, xnr=xnr: prenorm(xn[:, b, :], xnr[b], b, 0)
            ffn(1, xt, xrs, NB, 2, 2, pre_done=True, after_block=nxt, hook_independent=True)
            pre_done = nxt is not None
            if t + 1 < NT:
                load_wd(0, spread=True)
            for b in range(NB):
                dstb = out_d[t * T + b * 128:t * T + (b + 1) * 128, :]
                deferred.append(lambda dstb=dstb, xt=xt, par=par, xrs=xrs, b=b: S.dma(
                    "pool", lambda e: e.dma_start(out=dstb, in_=xt[:, b, :]), osem[par], reads=[xrs[b]]))
        flush_deferred()
        for par in range(2):
            if osem[par].count:
                nc.sync.wait_ge(osem[par].h, osem[par].count)
                nc.gpsimd.wait_ge(osem[par].h, osem[par].count)
    return nc


def _consts(first_half):
    qi = np.arange(128)[None, :]
    kj = np.arange(128)[:, None]
    cur = np.where(kj <= qi, 0.0, NEG).astype(np.float32)
    prev = np.where(kj > qi, 0.0, NEG).astype(np.float32)
    prev0 = np.full((128, 128), NEG, np.float32) if first_half else prev
    masks = np.stack([np.tile(m, (1, 4)) for m in (cur, prev, prev0)], axis=1)
    bands = np.zeros((128, 4, 4, 128), np.float32)
    ti = np.arange(128)[:, None]
    to = np.arange(128)[None, :]
    for g, w in enumerate(POOL_W):
        d = to - ti
        bcur = np.where((d >= 0) & (d <= w - 1), 1.0 / w, 0.0) - (d == 0)
        dp = to + 128 - ti
        bprev = np.where((dp >= 0) & (dp <= w - 1), 1.0 / w, 0.0)
        cnt = np.minimum(to + 1, w).astype(np.float64)
        bcur0 = np.where((d >= 0) & (d <= w - 1), 1.0 / cnt, 0.0) - (d == 0)
        bands[:, 0, g] = bcur
        bands[:, 1, g] = bprev
        bands[:, 2, g] = bcur0 if first_half else bcur
        bands[:, 3, g] = 0.0 if first_half else bprev
    return masks.astype(np.float32), bands.reshape(128, 4, 512).astype(np.float32)


_NC_CACHE = {}


def _run(inputs, NT=NT_FULL):
    x = np.ascontiguousarray(np.asarray(inputs["x"], dtype=np.float32))
    positions = np.asarray(inputs["positions"]).astype(np.int32)
    f = lambda k: np.ascontiguousarray(np.asarray(inputs[k], dtype=np.float32))
    B = x.shape[0]
    rep = lambda v: np.ascontiguousarray(np.broadcast_to(v[None, :], (128, v.shape[0])))
    gpost = np.ascontiguousarray(np.stack([rep(f("ffn1_post")[0]), rep(f("mix_post")[0]), rep(f("ffn2_post")[0])], axis=1))
    gpreT = np.ascontiguousarray(np.stack([f(k)[0].reshape(NKC, 128).T for k in ("ffn1_pre", "mix_pre", "ffn2_pre")], axis=1))
    pgs = np.ascontiguousarray(np.stack([rep(f("pool_scale")[0]), rep(f("g_pool")[0]), rep(f("g_attn")[0])], axis=1))
    sinks = rep(f("sinks")[0])
    inv_freq = (500000.0 ** (-np.arange(0, 16, 2, dtype=np.float32) / np.float32(16))).astype(np.float32)
    invf = rep(inv_freq)
    ident = np.eye(128, dtype=np.float32)
    shared = {
        "wgu1": f("ffn1_w_gu")[0], "wgu2": f("ffn2_w_gu")[0], "wd1": f("ffn1_w_down")[0], "wd2": f("ffn2_w_down")[0],
        "win": f("w_in")[0], "wout": f("w_out")[0], "wpool": f("w_pool")[0],
        "gpost": gpost, "gpreT": gpreT, "pgs": pgs, "sinks": sinks, "invf": invf, "ident": ident,
    }
    in_maps = []
    for core in range(8):
        b, half = core // 2, core % 2
        s0 = half * NTOK
        xs = np.zeros((128 + NTOK, D), np.float32)
        ps_ = np.zeros((128 + NTOK,), np.int32)
        xs[128:] = x[b, s0:s0 + NTOK]
        ps_[128:] = positions[b, s0:s0 + NTOK]
        if half == 1:
            xs[:128] = x[b, s0 - 128:s0]
            ps_[:128] = positions[b, s0 - 128:s0]
        masks, bands = _consts(half == 0)
        m = dict(shared)
        m.update({"x": xs, "pos": np.ascontiguousarray(ps_.reshape(33, 128).T), "masks": masks, "bands": bands})
        in_maps.append(m)
    if inputs.get("_only_core") is not None:
        c = inputs["_only_core"]
        res = run_bass_kernel_spmd(_NC_CACHE[NT], [in_maps[c]], core_ids=[0])
        return res.results[0]["out"]
    if NT not in _NC_CACHE:
        _NC_CACHE[NT] = build(NT)
    res = run_bass_kernel_spmd(_NC_CACHE[NT], in_maps, core_ids=list(range(8)))
    out = np.zeros((B, SEQ, D), np.float32)
    for core in range(8):
        b, half = core // 2, core % 2
        out[b, half * NTOK:(half + 1) * NTOK] = res.results[core]["out"]
    return out


def kernel(**inputs):
    return _run(inputs, NT_FULL)
```

```python
import numpy as np
from contextlib import ExitStack
import concourse.bass as bass
import concourse.mybir as mybir
from concourse.bass_utils import run_bass_kernel_spmd

F32 = mybir.dt.float32
BF16 = mybir.dt.bfloat16
I32 = mybir.dt.int32
AF = mybir.ActivationFunctionType
ALU = mybir.AluOpType

D = 1024
DFF = 2816
NFC = 22
NKC = 8
NB = 4
T = 512
NT_FULL = 8
SEQ = 8192
NTOK = 4096
EPS = 1e-6
NEG = -30000.0
POOL_W = (2, 4, 8, 16)
NSLOT = NB + 1
RING = 3
TWO_PI = 6.283185307179586


class Sem:
    def __init__(self, h):
        self.h = h
        self.count = 0


class Res:
    __slots__ = ("name", "w", "r", "excl")

    def __init__(self, name, excl=False):
        self.name = name
        self.w = None
        self.r = {}
        self.excl = excl


class Eng:
    def __init__(self, eng, sem, is_pe=False):
        self.eng = eng
        self.sem = sem
        self.is_pe = is_pe
        self.waited = {}


class Sched:
    def __init__(self):
        self.engs = {}

    def deps(self, E, reads, writes):
        need = {}

        def add(s, v):
            if need.get(s, 0) < v:
                need[s] = v

        for b in reads:
            if b.w is not None:
                add(*b.w)
            if b.excl:
                for s, v in b.r.items():
                    if s is not E.sem:
                        add(s, v)
        for b in writes:
            if b.w is not None:
                add(*b.w)
            for s, v in b.r.items():
                add(s, v)
        for s, v in need.items():
            if s is E.sem and E.is_pe:
                continue
            if E.waited.get(s, 0) >= v:
                continue
            E.eng.wait_ge(s.h, v)
            E.waited[s] = v

    def _mark(self, tag, reads, writes):
        s, v = tag
        for b in writes:
            b.w = tag
            b.r = {}
        for b in reads:
            if b not in writes:
                if b.r.get(s, 0) < v:
                    b.r[s] = v

    def op(self, ename, fn, reads=(), writes=(), signal=True):
        E = self.engs[ename]
        self.deps(E, reads, writes)
        inst = fn(E.eng)
        if signal:
            E.sem.count += 1
            inst.then_inc(E.sem.h, 1)
            tag = (E.sem, E.sem.count)
        else:
            tag = (E.sem, E.sem.count + 1)
        self._mark(tag, reads, writes)
        return inst

    def dma(self, qname, fn, sem, reads=(), writes=()):
        E = self.engs[qname]
        self.deps(E, reads, writes)
        inst = fn(E.eng)
        sem.count += 16
        inst.then_inc(sem.h, 16)
        self._mark((sem, sem.count), reads, writes)
        return inst


class _Stop(Exception):
    pass


def build(NT=NT_FULL, stop=None):
    nc = bass.Bass("TRN2", target_bir_lowering=False)
    es = ExitStack()
    try:
        _body(nc, es, NT, stop)
    except _Stop:
        pass
    es.close()
    return nc


def _body(nc, es, NT, stop):
    def stop_at(name):
        if stop == name:
            raise _Stop()

    def din(name, shape, dt=F32):
        return nc.dram_tensor(name, shape, dt, kind="ExternalInput").ap()

    x_d = din("x", [128 + NTOK, D])
    pos_d = din("pos", [128, 33], I32)
    wgu_d = [din("wgu1", [D, 2 * DFF]), din("wgu2", [D, 2 * DFF])]
    wd_d = [din("wd1", [DFF, D]), din("wd2", [DFF, D])]
    win_d = din("win", [D, 1280])
    wout_d = din("wout", [D, D])
    wpool_d = din("wpool", [4, 128, 128])
    gpost_d = din("gpost", [128, 3, D])
    gpre_d = din("gpreT", [128, 3, NKC])
    pgs_d = din("pgs", [128, 3, 512])
    sinks_d = din("sinks", [128, 8])
    invf_d = din("invf", [128, 8])
    masks_d = din("masks", [128, 3, 512])
    bands_d = din("bands", [128, 4, 512])
    ident_d = din("ident", [128, 128])
    out_d = nc.dram_tensor("out", [NTOK, D], F32, kind="ExternalOutput").ap()

    def dscr(name, shape):
        return nc.dram_tensor(name, shape, BF16, kind="Internal").ap()

    wgu_b = [dscr("wgu1b", [D, 2 * DFF]), dscr("wgu2b", [D, 2 * DFF])]
    wd_b = [dscr("wd1b", [DFF, D]), dscr("wd2b", [DFF, D])]
    win_b = dscr("winb", [D, 1280])
    wout_b = dscr("woutb", [D, D])
    wpool_b = dscr("wpoolb", [4, 128, 128])

    if True:
        def sb(name, shape, dt):
            return es.enter_context(nc.sbuf_tensor("sb_" + name, shape, dt))

        def new_sem(name):
            return Sem(es.enter_context(nc.semaphore(name)))

        S = Sched()
        S.engs["pe"] = Eng(nc.tensor, new_sem("s_pe"), is_pe=True)
        S.engs["act"] = Eng(nc.scalar, new_sem("s_act"))
        S.engs["dve"] = Eng(nc.vector, new_sem("s_dve"))
        S.engs["sp"] = Eng(nc.sync, None)
        S.engs["pool"] = Eng(nc.gpsimd, new_sem("s_pool"))

        xbuf = [sb(f"xbuf{i}", [128, NB, D], F32) for i in range(2)]
        xres = [[Res(f"x{i}_{b}") for b in range(NB)] for i in range(2)]
        hb = sb("hb", [128, D], BF16)
        r_hb = Res("hb")
        hT = sb("hT", [128, NKC, T], BF16)
        r_hT = [Res(f"hT{b}") for b in range(NB)]
        AT = sb("AT", [128, NFC, T], BF16)
        r_AT = [Res(f"AT{c}") for c in range(NFC)]
        ring = [sb(f"ring{i}", [128, NKC, 512], BF16) for i in range(RING)]
        r_ring = [Res(f"ring{i}") for i in range(RING)]
        ring_sem = [new_sem(f"s_ring{i}") for i in range(RING)]
        wd = sb("wd", [128, NFC, D], BF16)
        r_wd = Res("wd")
        wd_sem = new_sem("s_wd")
        gpost = sb("gpost", [128, 3, D], F32)
        gpreT = sb("gpreT", [128, 3, NKC], F32)
        pgs = sb("pgs", [128, 3, 512], F32)
        e_sb = [sb(f"e_sb{i}", [128, T], F32) for i in range(2)]
        r_e = [Res(f"e{i}") for i in range(2)]
        tpost = sb("tpost", [128, D], F32)
        r_tpost = Res("tpost")
        junk = sb("junk", [128, D], BF16)
        r_junk = Res("junk")
        r_junk2 = [Res("junk_a"), Res("junk_b")]
        stat = sb("stat", [128, 64], F32)
        u_tm = sb("u_tm", [128, NSLOT, 512], BF16)
        r_u = [Res(f"u{i}") for i in range(NSLOT)]
        qk_tm = [sb(f"qk_tm{i}", [128, 10, 64], BF16) for i in range(2)]
        r_qk = [Res(f"qk{i}") for i in range(2)]
        QT = sb("QT", [128, NB, 512], BF16)
        r_QT = [Res(f"QT{b}") for b in range(NB)]
        KT = sb("KT", [128, NSLOT * 128], BF16)
        r_KT = [Res(f"KT{i}") for i in range(NSLOT)]
        Vx = sb("Vx", [128, NSLOT, 2, 65], BF16)
        r_V = [Res(f"V{i}") for i in range(NSLOT)]
        PT = [sb(f"PT{g}", [128, 2, 512], BF16) for g in range(2)]
        r_PT = [Res(f"PT{g}") for g in range(2)]
        dT_sb = sb("dT_sb", [128, 4, 128], BF16)
        r_dT = Res("dT")
        t1 = sb("t1", [128, 512], F32)
        r_t1 = Res("t1")
        a_sb = sb("a_sb", [128, 512], F32)
        r_a = Res("a")
        ycat = [sb(f"ycat{i}", [128, D], BF16) for i in range(2)]
        r_ycat = [Res(f"ycat{i}") for i in range(2)]
        den_sb = sb("den_sb", [128, 8], F32)
        r_den = Res("den")
        rope_t = sb("rope_t", [128, 4, 10, 8], F32)
        r_rope = [[Res(f"rope{i}k"), Res(f"rope{i}q")] for i in range(4)]
        masks = sb("masks_b", [128, 3, 512], BF16)
        bands = sb("bands_b", [128, 4, 512], BF16)
        ident_f = sb("ident_f", [128, 128], F32)
        ident = sb("ident_b", [128, 128], BF16)
        wpool = sb("wpool", [128, 4, 128], BF16)
        sinks_sb = sb("sinks_sb", [128, 8], F32)
        esink = sb("esink", [128, 8], F32)
        invf = sb("invf", [128, 8], F32)
        pos_i = sb("pos_i", [128, 33], I32)
        pos_f = sb("pos_f", [128, 33], F32)
        ang = sb("ang", [128, 33, 8], F32)
        ang2 = sb("ang2", [128, 33, 8], F32)
        kf = sb("kf", [128, 33, 8], F32)
        ki = sb("ki", [128, 33, 8], I32)
        cos_t = sb("cos_t", [128, 33, 8], F32)
        sin_t = sb("sin_t", [128, 33, 8], F32)
        r_const = Res("const")
        ps = es.enter_context(nc.psum_tensor("ps", [128, 8 * 512], F32))
        r_bank = [Res(f"bank{i}", excl=True) for i in range(8)]

        def bank(i, n=1):
            return ps[:, i * 512:(i + n) * 512]

        def stat_col(i):
            return stat[:, i:i + 1]

        conv = {}

        def cast_rows(name, dst, src, nrows, step):
            s_ = new_sem("s_cv_" + name)
            r = Res("cv_" + name)
            for r0 in range(0, nrows, step):
                r1 = min(nrows, r0 + step)
                S.dma("pool", lambda e, r0=r0, r1=r1: e.dma_start(out=dst[r0:r1, :], in_=src[r0:r1, :]), s_, writes=[r])
            conv[name] = r

        def cast_wgu(f):
            cast_rows(f"gu{f}", wgu_b[f], wgu_d[f], D, 128)
            for i in range(4):
                conv[f"gu{f}_{i}"] = conv[f"gu{f}"]

        cast_rows("wpool", wpool_b.rearrange("g c d -> (g c) d"), wpool_d.rearrange("g c d -> (g c) d"), 512, 512)
        cast_wgu(0)
        cast_rows("wd0", wd_b[0], wd_d[0], DFF, 704)
        cast_rows("win", win_b, win_d, D, 512)
        cast_rows("wout", wout_b, wout_d, D, 512)
        cast_wgu(1)
        cast_rows("wd1", wd_b[1], wd_d[1], DFF, 704)

        csem = new_sem("s_const")
        for dst, src in ((gpost[:], gpost_d), (gpreT[:], gpre_d), (pgs[:], pgs_d), (sinks_sb[:], sinks_d),
                         (invf[:], invf_d), (ident_f[:], ident_d), (pos_i[:], pos_d)):
            S.dma("sp", lambda e, dst=dst, src=src: e.dma_start(out=dst, in_=src), csem, writes=[r_const])
        stage_m = [(xbuf[1][:, i, 0:512], xres[1][i]) for i in range(3)]
        stage_b = [(xbuf[1][:, i, 512:1024], xres[1][i]) for i in range(4)]
        for i, (dst, rs_) in enumerate(stage_m):
            S.dma("sp", lambda e, dst=dst, i=i: e.dma_start(out=dst, in_=masks_d[:, i, :]), csem, writes=[r_const, rs_])
        for i, (dst, rs_) in enumerate(stage_b):
            S.dma("sp", lambda e, dst=dst, i=i: e.dma_start(out=dst, in_=bands_d[:, i, :]), csem, writes=[r_const, rs_])
        wpsem = new_sem("s_wpool")
        r_wpool = Res("wpool")
        S.dma("sp", lambda e: e.dma_start(out=wpool[:], in_=wpool_b.rearrange("g c d -> c g d")), wpsem,
              reads=[conv["wpool"]], writes=[r_wpool])

        r_c2 = Res("const2")
        for i, (src, rs_) in enumerate(stage_m):
            S.op("dve", lambda e, i=i, src=src: e.tensor_copy(out=masks[:, i, :], in_=src), reads=[r_const, rs_],
                 writes=[r_c2])
        for i, (src, rs_) in enumerate(stage_b):
            S.op("dve", lambda e, i=i, src=src: e.tensor_copy(out=bands[:, i, :], in_=src), reads=[r_const, rs_],
                 writes=[r_c2])
        S.op("dve", lambda e: e.tensor_copy(out=ident[:], in_=ident_f[:]), reads=[r_const], writes=[r_c2])
        S.op("dve", lambda e: e.memset(Vx[:], 1.0), writes=r_V)
        S.op("dve", lambda e: e.tensor_scalar(out=gpreT[:], in0=gpreT[:], scalar1=32.0, scalar2=None, op0=ALU.mult),
             reads=[r_const], writes=[r_c2])
        for i, c in enumerate((16.0, 32.0, 16.0)):
            S.op("dve", lambda e, i=i, c=c: e.tensor_scalar(out=gpost[:, i, :], in0=gpost[:, i, :], scalar1=c,
                                                            scalar2=None, op0=ALU.mult), reads=[r_const], writes=[r_c2])
        s512 = float(np.sqrt(512.0))
        for i in (1, 2):
            S.op("dve", lambda e, i=i: e.tensor_scalar(out=pgs[:, i, :], in0=pgs[:, i, :], scalar1=s512,
                                                       scalar2=None, op0=ALU.mult), reads=[r_const], writes=[r_c2])
        r_ang = Res("ang")
        S.op("dve", lambda e: e.tensor_copy(out=pos_f[:], in_=pos_i[:]), reads=[r_const], writes=[r_ang])
        for f in range(8):
            S.op("dve", lambda e, f=f: e.tensor_scalar(out=ang[:, :, f], in0=pos_f[:], scalar1=invf[:, f:f + 1],
                                                      scalar2=None, op0=ALU.mult), reads=[r_const, r_ang], writes=[r_ang])

        def reduce_sin(src, dst):
            S.op("dve", lambda e: e.tensor_scalar(out=kf[:], in0=src[:], scalar1=1.0 / TWO_PI, scalar2=None,
                                                  op0=ALU.mult), reads=[r_ang], writes=[r_ang])
            S.op("dve", lambda e: e.tensor_copy(out=ki[:], in_=kf[:]), reads=[r_ang], writes=[r_ang])
            S.op("dve", lambda e: e.tensor_copy(out=kf[:], in_=ki[:]), reads=[r_ang], writes=[r_ang])
            S.op("dve", lambda e: e.scalar_tensor_tensor(out=src[:], in0=kf[:], scalar=-6.28125, in1=src[:],
                                                         op0=ALU.mult, op1=ALU.add), reads=[r_ang], writes=[r_ang])
            S.op("dve", lambda e: e.scalar_tensor_tensor(out=src[:], in0=kf[:], scalar=-0.0019353071795864769,
                                                         in1=src[:], op0=ALU.mult, op1=ALU.add),
                 reads=[r_ang], writes=[r_ang])
            S.op("dve", lambda e: e.tensor_scalar(out=kf[:], in0=src[:], scalar1=3.141592653589793,
                                                  scalar2=-TWO_PI, op0=ALU.is_gt, op1=ALU.mult),
                 reads=[r_ang], writes=[r_ang])
            S.op("dve", lambda e: e.tensor_tensor(out=src[:], in0=src[:], in1=kf[:], op=ALU.add),
                 reads=[r_ang], writes=[r_ang])
            S.op("dve", lambda e: e.tensor_scalar(out=kf[:], in0=src[:], scalar1=-3.141592653589793,
                                                  scalar2=TWO_PI, op0=ALU.is_lt, op1=ALU.mult),
                 reads=[r_ang], writes=[r_ang])
            S.op("dve", lambda e: e.tensor_tensor(out=src[:], in0=src[:], in1=kf[:], op=ALU.add),
                 reads=[r_ang], writes=[r_ang])
            S.op("act", lambda e: e.activation(out=dst[:], in_=src[:], func=AF.Sin), reads=[r_ang], writes=[r_c2])

        S.op("dve", lambda e: e.tensor_scalar(out=ang2[:], in0=ang[:], scalar1=1.5707963267948966, scalar2=None,
                                              op0=ALU.add), reads=[r_ang], writes=[r_ang])
        reduce_sin(ang, sin_t)
        reduce_sin(ang2, cos_t)
        S.op("act", lambda e: e.activation(out=esink[:], in_=sinks_sb[:], func=AF.Exp), reads=[r_const], writes=[r_c2])

        stop_at("consts")
        ring_items = []
        ring_state = {"loaded": 0, "consumed": 0}
        item_done = []

        def wgu_item(f, j):
            sg = j // 3

            def mk(gu):
                src = wgu_b[f].rearrange("(kc p) c -> p kc c", p=128)[:, :, gu * DFF + j * 256:gu * DFF + (j + 1) * 256]
                return lambda e, slot: e.dma_start(out=ring[slot][:, :, gu * 256:(gu + 1) * 256], in_=src)
            return ([mk(0), mk(1)], [conv[f"gu{f}_{sg}"]])

        def wmat_item(wb, lo, hi, cname):
            src = wb.rearrange("(kc p) c -> p kc c", p=128)[:, :, lo:hi]
            return ([lambda e, slot: e.dma_start(out=ring[slot][:, :, 0:hi - lo], in_=src)], [conv[cname]])

        def ring_prefetch():
            while ring_state["loaded"] < len(ring_items) and ring_state["loaded"] - ring_state["consumed"] < RING:
                k = ring_state["loaded"]
                slot = k % RING
                fns, reads = ring_items[k]
                for fn in fns:
                    S.dma("sp", lambda e, fn=fn, slot=slot: fn(e, slot), ring_sem[slot], reads=reads,
                          writes=[r_ring[slot]])
                ring_state["loaded"] += 1

        ring_pos = {"next": 0}

        def ring_take(n=1):
            k = ring_pos["next"]
            ring_pos["next"] += n
            assert ring_state["loaded"] >= k + n, "ring underflow"
            return [(k + i) % RING for i in range(n)]

        def ring_release(n=1):
            ring_state["consumed"] += n
            ring_prefetch()

        wd_state = {"content": None}

        WD_PIECES = [(0, 3), (3, 6), (6, 9), (9, 12), (12, 15), (15, 18), (18, 20), (20, 22)]
        deferred = []

        def flush_deferred(n=None):
            while deferred and (n is None or n > 0):
                deferred.pop(0)()
                if n is not None:
                    n -= 1

        def wd_piece(f, lo, hi):
            src = wd_b[f].rearrange("(fc p) d -> p fc d", p=128)[:, lo:hi, :]
            S.dma("pool", lambda e: e.dma_start(out=wd[:, lo:hi, :], in_=src), wd_sem,
                  reads=[conv[f"wd{f}"]], writes=[r_wd])

        def load_wd(f, spread=False):
            if wd_state["content"] == f:
                return
            wd_state["content"] = f
            for lo, hi in WD_PIECES:
                if spread:
                    deferred.append(lambda f=f, lo=lo, hi=hi: wd_piece(f, lo, hi))
                else:
                    wd_piece(f, lo, hi)

        stat_i = {"n": 0}

        def new_stat():
            i = stat_i["n"] % 64
            stat_i["n"] += 1
            return i, Res(f"stat{stat_i['n']}")

        def rstd_from(src_ap, src_res, n_eps, width):
            pieces = src_ap if isinstance(src_ap, list) else [src_ap]
            cols = []
            for pi_, pap in enumerate(pieces):
                i, r = new_stat()
                col = stat_col(i)
                w_ = width // len(pieces)
                S.op("act", lambda e, pap=pap, col=col, w_=w_, pi_=pi_: e.activation(
                    out=junk[:, pi_ * 512:pi_ * 512 + w_], in_=pap, func=AF.Square, accum_out=col),
                    reads=src_res, writes=[r, r_junk2[pi_]])
                cols.append((col, r))
            col, r = cols[0]
            if len(cols) == 2:
                i, r_s = new_stat()
                cs_ = stat_col(i)
                S.op("act", lambda e: e.activation(out=cs_, in_=cols[0][0], func=AF.Identity, bias=cols[1][0]),
                     reads=[cols[0][1], cols[1][1]], writes=[r_s])
                col, r = cs_, r_s
            i2, r2 = new_stat()
            col2 = stat_col(i2)
            S.op("act", lambda e: e.activation(out=col2, in_=col, func=AF.Ln, bias=eps_col[n_eps]),
                 reads=[r, r_c2], writes=[r2])
            i3, r3 = new_stat()
            col3 = stat_col(i3)
            S.op("act", lambda e: e.activation(out=col3, in_=col2, func=AF.Exp, scale=-0.5), reads=[r2], writes=[r3])
            return col3, r3

        epst = sb("epst", [128, 4], F32)
        S.op("dve", lambda e: e.memset(epst[:, 0:1], 1024 * EPS), writes=[r_c2])
        S.op("dve", lambda e: e.memset(epst[:, 1:2], 512 * EPS), writes=[r_c2])
        S.op("dve", lambda e: e.memset(epst[:, 2:3], 1.0), writes=[r_c2])
        eps_col = {1024: epst[:, 0:1], 512: epst[:, 1:2]}
        one_col = epst[:, 2:3]

        tp_bf = bank(0).bitcast(BF16)

        def prenorm_stats(xb, xr):
            rs, rsr = rstd_from(xb, [xr], 1024, D)
            S.op("dve", lambda e: e.tensor_scalar(out=hb[:], in0=xb, scalar1=rs, scalar2=None, op0=ALU.mult),
                 reads=[xr, rsr], writes=[r_hb])

        def prenorm_T(b, gi):
            for kc in range(NKC):
                S.op("pe", lambda e, kc=kc: e.transpose(out=tp_bf[:, kc * 128:(kc + 1) * 128],
                                                        in_=hb[:, kc * 128:(kc + 1) * 128], identity=ident[:]),
                     reads=[r_hb, r_c2], writes=[r_bank[0]], signal=(kc == NKC - 1))
            gb = gpreT[:, gi, :].unsqueeze(2).to_broadcast([128, NKC, 128])
            S.op("dve", lambda e: e.tensor_tensor(out=hT[:, :, b * 128:(b + 1) * 128],
                                                  in0=tp_bf.rearrange("p (k t) -> p k t", k=NKC), in1=gb, op=ALU.mult),
                 reads=[r_bank[0], r_c2], writes=[r_hT[b]])

        def prenorm(xb, xr, b, gi):
            prenorm_stats(xb, xr)
            prenorm_T(b, gi)

        def postnorm(ybanks, xb, xr, gi):
            yres = [r_bank[ybanks], r_bank[ybanks + 1]]
            rs, rsr = rstd_from([bank(ybanks), bank(ybanks + 1)], yres, 1024, D)
            for h in range(2):
                S.op("dve", lambda e, h=h: e.scalar_tensor_tensor(
                    out=tpost[:, h * 512:(h + 1) * 512], in0=bank(ybanks + h), scalar=rs,
                    in1=gpost[:, gi, h * 512:(h + 1) * 512], op0=ALU.mult, op1=ALU.mult),
                    reads=[yres[h], rsr, r_c2], writes=[r_tpost])
            S.op("dve", lambda e: e.tensor_tensor(out=xb, in0=tpost[:], in1=xb, op=ALU.add),
                 reads=[r_tpost], writes=[xr])

        def ffn_items(f):
            return [wgu_item(f, j) for j in range(11)]

        def ffn(f, xt, xrs, nb, gpre_i, gpost_i, pre_done=False, after_block=None, final_hook=True,
                hook_independent=False):
            N = nb * 128
            if not pre_done:
                for b in range(nb):
                    prenorm(xt[:, b, :], xrs[b], b, gpre_i)
            stop_at("prenorm")
            pp = 0
            for j in range(11):
                if j >= 1:
                    flush_deferred(2)
                slot = ring_take()[0]
                for s in range(2):
                    c = 2 * j + s
                    gb_, ub_ = 2 * pp, 2 * pp + 1
                    pp ^= 1
                    for which, bk in ((0, gb_), (1, ub_)):
                        for kc in range(NKC):
                            S.op("pe", lambda e, kc=kc, which=which, bk=bk: e.matmul(
                                bank(bk)[:, 0:N], lhsT=ring[slot][:, kc, which * 256 + s * 128:which * 256 + (s + 1) * 128],
                                rhs=hT[:, kc, 0:N], start=(kc == 0), stop=(kc == NKC - 1)),
                                reads=[r_ring[slot]] + r_hT[:nb], writes=[r_bank[bk]], signal=(kc == NKC - 1))
                    ei = c % 2
                    S.op("act", lambda e, ei=ei, gb_=gb_: e.activation(out=e_sb[ei][:, 0:N], in_=bank(gb_)[:, 0:N],
                                                                      func=AF.Silu),
                         reads=[r_bank[gb_]], writes=[r_e[ei]])
                    S.op("dve", lambda e, ub_=ub_, c=c, ei=ei: e.tensor_tensor(out=AT[:, c, 0:N], in0=bank(ub_)[:, 0:N],
                                                                              in1=e_sb[ei][:, 0:N], op=ALU.mult),
                         reads=[r_bank[ub_], r_e[ei]], writes=[r_AT[c]])
                ring_release()
            stop_at("up")
            flush_deferred()
            for b in range(nb):
                yb = 4 + 2 * (b % 2)
                for half in range(2):
                    for fc in range(NFC):
                        S.op("pe", lambda e, fc=fc, half=half, yb=yb, b=b: e.matmul(
                            bank(yb + half), lhsT=AT[:, fc, b * 128:(b + 1) * 128],
                            rhs=wd[:, fc, half * 512:(half + 1) * 512], start=(fc == 0), stop=(fc == NFC - 1)),
                            reads=r_AT + [r_wd], writes=[r_bank[yb + half]], signal=(fc == NFC - 1))
                if after_block is not None and b >= 1:
                    after_block(b - 1)
                if after_block is not None and hook_independent and final_hook and b == nb - 1:
                    after_block(nb - 1)
                postnorm(yb, xt[:, b, :], xrs[b], gpost_i)
            if after_block is not None and final_hook and not hook_independent:
                after_block(nb - 1)

        class Mixer:
            def __init__(self, xt, xrs, base, halo, blk0, first_core_block):
                self.xt, self.xrs, self.base, self.halo, self.blk0, self.first = xt, xrs, base, halo, blk0, first_core_block

            def take_in(self):
                self.su, self.sq, self.skv = ring_take(3)

            def A(self, b):
                prenorm(self.xt[:, b, :], self.xrs[b], b, 1)

            def B1(self, b):
                halo = self.halo
                sl = (self.base + 1 + b) % NSLOT
                tok = slice(b * 128, (b + 1) * 128)
                groups = [(self.su, 1, 512), (self.skv, 3, 256)] + ([] if halo else [(self.sq, 2, 512)])
                for slot, bk, ncol in groups:
                    for kc in range(NKC):
                        S.op("pe", lambda e, kc=kc, slot=slot, bk=bk, ncol=ncol: e.matmul(
                            bank(bk)[:, 0:ncol], lhsT=hT[:, kc, tok], rhs=ring[slot][:, kc, 0:ncol],
                            start=(kc == 0), stop=(kc == NKC - 1)),
                            reads=[r_ring[slot], r_hT[b]], writes=[r_bank[bk]], signal=(kc == NKC - 1))
                S.op("act", lambda e: e.activation(out=u_tm[:, sl, :], in_=bank(1), func=AF.Identity),
                     reads=[r_bank[1]], writes=[r_u[sl]])
                S.op("act", lambda e: e.activation(out=Vx[:, sl, :, 0:64],
                                                   in_=bank(3)[:, 128:256].rearrange("p (g d) -> p g d", g=2),
                                                   func=AF.Identity), reads=[r_bank[3]], writes=[r_V[sl]])
                blk = self.blk0 + b
                qk_t = qk_tm[b % 2]
                r_qk_ = r_qk[b % 2]
                parts = [(bank(3)[:, 0:128].rearrange("p (h d) -> p h d", d=64), 2, r_bank[3], 8)]
                if not halo:
                    parts.append((bank(2).rearrange("p (h d) -> p h d", d=64), 8, r_bank[2], 0))
                for zz, nh, zr_, h0 in parts:
                    cb = cos_t[:, blk, :].unsqueeze(1).to_broadcast([128, nh, 8])
                    sn = sin_t[:, blk, :].unsqueeze(1).to_broadcast([128, nh, 8])
                    x1, x2 = zz[:, :, 0:8], zz[:, :, 8:16]
                    tmp = [rope_t[:, i, h0:h0 + nh, :] for i in range(4)]
                    rr_ = [r_rope[i][0 if nh == 2 else 1] for i in range(4)]
                    for i, (xa, tb_) in enumerate(((x1, cb), (x2, sn), (x2, cb), (x1, sn))):
                        S.op("dve", lambda e, i=i, xa=xa, tb_=tb_: e.tensor_tensor(out=tmp[i], in0=xa, in1=tb_, op=ALU.mult),
                             reads=[zr_, r_c2], writes=[rr_[i]])
                    if nh == 2:
                        qo, zi, tm = qk_t[:, 8:10, :], zz, tmp
                    else:
                        pq = lambda ap: ap.rearrange("p (g c) d -> p g c d", g=2)
                        qo, zi, tm = qk_t[:, 0:8, :].rearrange("p (c g) d -> p g c d", g=2), pq(zz), [pq(t_) for t_ in tmp]
                    S.op("dve", lambda e: e.tensor_tensor(out=qo[..., 0:8], in0=tm[0], in1=tm[1], op=ALU.subtract),
                         reads=[rr_[0], rr_[1]], writes=[r_qk_])
                    S.op("dve", lambda e: e.tensor_tensor(out=qo[..., 8:16], in0=tm[2], in1=tm[3], op=ALU.add),
                         reads=[rr_[2], rr_[3]], writes=[r_qk_])
                    S.op("act", lambda e: e.activation(out=qo[..., 16:64], in_=zi[..., 16:64], func=AF.Identity),
                         reads=[zr_], writes=[r_qk_])

            def B2(self, b):
                halo = self.halo
                sl = (self.base + 1 + b) % NSLOT
                qk_t = qk_tm[b % 2]
                r_qk_ = r_qk[b % 2]
                tpq = tp_bf
                if not halo:
                    for c in range(4):
                        S.op("pe", lambda e, c=c: e.transpose(
                            out=tpq[:, c * 128:(c + 1) * 128],
                            in_=qk_t[:, 2 * c:2 * c + 2, :].rearrange("p h d -> p (h d)"), identity=ident[:]),
                             reads=[r_qk_, r_c2], writes=[r_bank[0]], signal=False)
                S.op("pe", lambda e: e.transpose(out=tpq[:, 512:640], in_=qk_t[:, 8:10, :].rearrange("p h d -> p (h d)"),
                                                 identity=ident[:]), reads=[r_qk_, r_c2], writes=[r_bank[0]])
                if not halo:
                    S.op("act", lambda e: e.activation(out=QT[:, b, :], in_=tpq[:, 0:512], func=AF.Identity),
                         reads=[r_bank[0]], writes=[r_QT[b]])
                S.op("act", lambda e: e.activation(out=KT[:, sl * 128:(sl + 1) * 128], in_=tpq[:, 512:640],
                                                   func=AF.Identity), reads=[r_bank[0]], writes=[r_KT[sl]])

            def C1(self, b):
                sl = (self.base + 1 + b) % NSLOT
                slp = (self.base + b) % NSLOT
                tok = slice(b * 128, (b + 1) * 128)
                first = self.first and b == 0
                mprev = 2 if first else 1
                ycat_, r_ycat_ = ycat[b % 2], r_ycat[b % 2]
                for g in range(2):
                    pr = slice(g * 64, (g + 1) * 64)
                    for kb, (ksl, mi) in enumerate(((slp, mprev), (sl, 0))):
                        bk = 4 + 2 * g + kb
                        S.op("pe", lambda e, bk=bk, ksl=ksl, pr=pr: e.matmul(
                            bank(bk), lhsT=KT[pr, ksl * 128:(ksl + 1) * 128], rhs=QT[pr, b, :], start=True, stop=False),
                            reads=[r_KT[ksl], r_QT[b]], writes=[r_bank[bk]], signal=False)
                        S.op("pe", lambda e, bk=bk, mi=mi: e.matmul(bank(bk), lhsT=ident[:], rhs=masks[:, mi, :],
                                                                    start=False, stop=True),
                             reads=[r_c2], writes=[r_bank[bk]])
                        S.op("act", lambda e, bk=bk, g=g, kb=kb: e.activation(out=PT[g][:, kb, :], in_=bank(bk),
                                                                              func=AF.Exp, scale=0.125),
                             reads=[r_bank[bk]], writes=[r_PT[g]])
                bc, bp = (2, 3) if first else (0, 1)
                for g in range(4):
                    cs = slice(g * 128, (g + 1) * 128)
                    S.op("pe", lambda e, cs=cs: e.matmul(bank(5)[:, cs], lhsT=u_tm[:, sl, cs], rhs=bands[:, bc, cs],
                                                         start=True, stop=False),
                         reads=[r_u[sl], r_c2], writes=[r_bank[5]], signal=False)
                    S.op("pe", lambda e, cs=cs: e.matmul(bank(5)[:, cs], lhsT=u_tm[64:128, slp, cs],
                                                         rhs=bands[64:128, bp, cs], start=False, stop=True),
                         reads=[r_u[slp], r_c2], writes=[r_bank[5]], signal=(g == 3))
                S.op("act", lambda e: e.activation(out=dT_sb[:], in_=bank(5).rearrange("p (g t) -> p g t", g=4),
                                                   func=AF.Identity), reads=[r_bank[5]], writes=[r_dT])
                for g in range(2):
                    ob = 4 + 2 * g
                    for c in range(4):
                        for kb, ksl in enumerate((slp, sl)):
                            S.op("pe", lambda e, g=g, c=c, kb=kb, ksl=ksl, ob=ob: e.matmul(
                                bank(ob)[:, c * 65:(c + 1) * 65], lhsT=PT[g][:, kb, c * 128:(c + 1) * 128],
                                rhs=Vx[:, ksl, g, :], start=(kb == 0), stop=(kb == 1)),
                                reads=[r_PT[g], r_V[ksl]], writes=[r_bank[ob]], signal=(c == 3 and kb == 1))
                for g in range(4):
                    cs = slice(g * 128, (g + 1) * 128)
                    S.op("pe", lambda e, g=g, cs=cs: e.matmul(bank(7)[:, cs], lhsT=dT_sb[:, g, :], rhs=wpool[:, g, :],
                                                              start=True, stop=True),
                         reads=[r_dT, r_wpool], writes=[r_bank[7]], signal=(g == 3))
                den = den_sb[:]
                for g in range(2):
                    ovg = bank(4 + 2 * g)[:, 0:260].rearrange("p (c d) -> p c d", c=4)
                    S.op("dve", lambda e, g=g, ovg=ovg: e.tensor_tensor(out=den_sb[:, g * 4:(g + 1) * 4], in0=ovg[:, :, 64],
                                                                       in1=esink[:, g * 4:(g + 1) * 4], op=ALU.add),
                         reads=[r_bank[4 + 2 * g], r_c2], writes=[r_den])
                S.op("dve", lambda e: e.reciprocal(out=den, in_=den), reads=[r_den], writes=[r_den])
                for g in range(2):
                    ovg = bank(4 + 2 * g)[:, 0:260].rearrange("p (c d) -> p c d", c=4)
                    S.op("dve", lambda e, g=g, ovg=ovg: e.tensor_tensor(
                        out=a_sb[:, g * 256:(g + 1) * 256].rearrange("p (c d) -> p c d", c=4), in0=ovg[:, :, 0:64],
                        in1=den_sb[:, g * 4:(g + 1) * 4].unsqueeze(2).to_broadcast([128, 4, 64]), op=ALU.mult),
                        reads=[r_bank[4 + 2 * g], r_den], writes=[r_a])
                S.op("dve", lambda e: e.tensor_tensor(out=t1[:], in0=bank(7), in1=pgs[:, 0, :], op=ALU.mult),
                     reads=[r_bank[7], r_c2], writes=[r_t1])
                rs_a, rsr_a = rstd_from(a_sb[:], [r_a], 512, 512)
                rs_p, rsr_p = rstd_from(t1[:], [r_t1], 512, 512)
                S.op("dve", lambda e: e.scalar_tensor_tensor(out=ycat_[:, 512:1024], in0=a_sb[:], scalar=rs_a,
                                                             in1=pgs[:, 2, :], op0=ALU.mult, op1=ALU.mult),
                     reads=[r_a, rsr_a, r_c2], writes=[r_ycat_])
                S.op("dve", lambda e: e.scalar_tensor_tensor(out=ycat_[:, 0:512], in0=t1[:], scalar=rs_p,
                                                             in1=pgs[:, 1, :], op0=ALU.mult, op1=ALU.mult),
                     reads=[r_t1, rsr_p, r_c2], writes=[r_ycat_])

            def C2(self, b):
                tok = slice(b * 128, (b + 1) * 128)
                ycat_, r_ycat_ = ycat[b % 2], r_ycat[b % 2]
                for kc in range(NKC):
                    S.op("pe", lambda e, kc=kc: e.transpose(out=tp_bf[:, kc * 128:(kc + 1) * 128],
                                                            in_=ycat_[:, kc * 128:(kc + 1) * 128], identity=ident[:]),
                         reads=[r_ycat_, r_c2], writes=[r_bank[0]], signal=(kc == NKC - 1))
                S.op("act", lambda e: e.activation(out=hT[:, :, tok], in_=tp_bf.rearrange("p (k t) -> p k t", k=NKC),
                                                   func=AF.Identity), reads=[r_bank[0]], writes=[r_hT[b]])

            def take_out(self):
                self.wslots = ring_take(2)

            def D(self, b):
                tok = slice(b * 128, (b + 1) * 128)
                yb = 4 + 2 * (b % 2)
                for half in range(2):
                    for kc in range(NKC):
                        S.op("pe", lambda e, kc=kc, half=half, yb=yb: e.matmul(
                            bank(yb + half), lhsT=hT[:, kc, tok], rhs=ring[self.wslots[half]][:, kc, :],
                            start=(kc == 0), stop=(kc == NKC - 1)),
                            reads=[r_hT[b], r_ring[self.wslots[half]]], writes=[r_bank[yb + half]],
                            signal=(kc == NKC - 1))
                postnorm(yb, self.xt[:, b, :], self.xrs[b], 1)

        def win_items():
            return [wmat_item(win_b, 0, 512, "win"), wmat_item(win_b, 512, 1024, "win"),
                    wmat_item(win_b, 1024, 1280, "win")]

        for t in range(max(NT, 1)):
            ring_items.extend(ffn_items(0))
            if t == 0:
                ring_items.extend(ffn_items(0))
                ring_items.extend(win_items())
            if NT > 0:
                ring_items.extend(win_items())
                ring_items.extend([wmat_item(wout_b, 0, 512, "wout"), wmat_item(wout_b, 512, 1024, "wout")])
                ring_items.extend(ffn_items(1))

        xsem = [new_sem("s_x0"), new_sem("s_x1")]
        osem = [new_sem("s_o0"), new_sem("s_o1")]

        def load_x(par, row0, nb):
            src = x_d[row0:row0 + nb * 128, :].rearrange("(b p) d -> p b d", p=128)
            S.dma("sp", lambda e: e.dma_start(out=xbuf[par][:, 0:nb, :], in_=src), xsem[par], writes=xres[par][:nb])

        load_x(0, 0, 1)
        if NT > 0:
            load_x(1, 128, NB)
        ring_prefetch()
        load_wd(0)

        def halo_pass():
            ffn(0, xbuf[0], xres[0], 1, 0, 0)
            mh = Mixer(xbuf[0], xres[0], NSLOT - 1, True, 0, False)
            mh.take_in()
            mh.A(0)
            mh.B1(0)
            mh.B2(0)
            ring_release(3)

        stop_at("loads")
        base = 0
        pre_done = False
        if NT == 0:
            ring_items[0:11] = []
            halo_pass()
        for t in range(NT):
            par = (t + 1) % 2
            xt, xrs = xbuf[par], xres[par]
            mx = Mixer(xt, xrs, base, False, 1 + t * NB, t == 0)
            if t == 0:
                ffn(0, xt, xrs, NB, 0, 0)
                halo_pass()
                for b in range(NB - 1):
                    mx.A(b)
            else:
                ffn(0, xt, xrs, NB, 0, 0, pre_done=pre_done, after_block=mx.A, final_hook=False)
            if t + 1 < NT:
                load_x(t % 2, 128 + (t + 1) * T, NB)
            load_wd(1)
            mx.take_in()
            for i in range(2 * NB + 2):
                if i < NB:
                    mx.B1(i)
                    if i == NB - 1:
                        ring_release(3)
                        mx.take_out()
                if i == 0:
                    prenorm_stats(xt[:, NB - 1, :], xrs[NB - 1])
                if 1 <= i <= NB:
                    mx.C1(i - 1)
                if i == 1:
                    prenorm_T(NB - 1, 1)
                if 2 <= i <= NB + 1:
                    mx.C2(i - 2)
                if i < NB:
                    mx.B2(i)
                if NB <= i < 2 * NB:
                    mx.D(i - NB)
                if NB + 2 <= i:
                    prenorm(xt[:, i - NB - 2, :], xrs[i - NB - 2], i - NB - 2, 2)
            ring_release(2)
            base = (base + NB) % NSLOT
            nxt = None
            if t + 1 < NT:
                xn, xnr = xbuf[t % 2], xres[t % 2]
                nxt = lambda b, xn=xn, xnr=xnr: prenorm(xn[:, b, :], xnr[b], b, 0)
            ffn(1, xt, xrs, NB, 2, 2, pre_done=True, after_block=nxt, hook_independent=True)
            pre_done = nxt is not None
            if t + 1 < NT:
                load_wd(0, spread=True)
            for b in range(NB):
                dstb = out_d[t * T + b * 128:t * T + (b + 1) * 128, :]
                deferred.append(lambda dstb=dstb, xt=xt, par=par, xrs=xrs, b=b: S.dma(
                    "pool", lambda e: e.dma_start(out=dstb, in_=xt[:, b, :]), osem[par], reads=[xrs[b]]))
        flush_deferred()
        for par in range(2):
            if osem[par].count:
                nc.sync.wait_ge(osem[par].h, osem[par].count)
                nc.gpsimd.wait_ge(osem[par].h, osem[par].count)
    return nc


def _consts(first_half):
    qi = np.arange(128)[None, :]
    kj = np.arange(128)[:, None]
    cur = np.where(kj <= qi, 0.0, NEG).astype(np.float32)
    prev = np.where(kj > qi, 0.0, NEG).astype(np.float32)
    prev0 = np.full((128, 128), NEG, np.float32) if first_half else prev
    masks = np.stack([np.tile(m, (1, 4)) for m in (cur, prev, prev0)], axis=1)
    bands = np.zeros((128, 4, 4, 128), np.float32)
    ti = np.arange(128)[:, None]
    to = np.arange(128)[None, :]
    for g, w in enumerate(POOL_W):
        d = to - ti
        bcur = np.where((d >= 0) & (d <= w - 1), 1.0 / w, 0.0) - (d == 0)
        dp = to + 128 - ti
        bprev = np.where((dp >= 0) & (dp <= w - 1), 1.0 / w, 0.0)
        cnt = np.minimum(to + 1, w).astype(np.float64)
        bcur0 = np.where((d >= 0) & (d <= w - 1), 1.0 / cnt, 0.0) - (d == 0)
        bands[:, 0, g] = bcur
        bands[:, 1, g] = bprev
        bands[:, 2, g] = bcur0 if first_half else bcur
        bands[:, 3, g] = 0.0 if first_half else bprev
    return masks.astype(np.float32), bands.reshape(128, 4, 512).astype(np.float32)


_NC_CACHE = {}


def _run(inputs, NT=NT_FULL):
    x = np.ascontiguousarray(np.asarray(inputs["x"], dtype=np.float32))
    positions = np.asarray(inputs["positions"]).astype(np.int32)
    f = lambda k: np.ascontiguousarray(np.asarray(inputs[k], dtype=np.float32))
    B = x.shape[0]
    rep = lambda v: np.ascontiguousarray(np.broadcast_to(v[None, :], (128, v.shape[0])))
    gpost = np.ascontiguousarray(np.stack([rep(f("ffn1_post")[0]), rep(f("mix_post")[0]), rep(f("ffn2_post")[0])], axis=1))
    gpreT = np.ascontiguousarray(np.stack([f(k)[0].reshape(NKC, 128).T for k in ("ffn1_pre", "mix_pre", "ffn2_pre")], axis=1))
    pgs = np.ascontiguousarray(np.stack([rep(f("pool_scale")[0]), rep(f("g_pool")[0]), rep(f("g_attn")[0])], axis=1))
    sinks = rep(f("sinks")[0])
    inv_freq = (500000.0 ** (-np.arange(0, 16, 2, dtype=np.float32) / np.float32(16))).astype(np.float32)
    invf = rep(inv_freq)
    ident = np.eye(128, dtype=np.float32)
    shared = {
        "wgu1": f("ffn1_w_gu")[0], "wgu2": f("ffn2_w_gu")[0], "wd1": f("ffn1_w_down")[0], "wd2": f("ffn2_w_down")[0],
        "win": f("w_in")[0], "wout": f("w_out")[0], "wpool": f("w_pool")[0],
        "gpost": gpost, "gpreT": gpreT, "pgs": pgs, "sinks": sinks, "invf": invf, "ident": ident,
    }
    in_maps = []
    for core in range(8):
        b, half = core // 2, core % 2
        s0 = half * NTOK
        xs = np.zeros((128 + NTOK, D), np.float32)
        ps_ = np.zeros((128 + NTOK,), np.int32)
        xs[128:] = x[b, s0:s0 + NTOK]
        ps_[128:] = positions[b, s0:s0 + NTOK]
        if half == 1:
            xs[:128] = x[b, s0 - 128:s0]
            ps_[:128] = positions[b, s0 - 128:s0]
        masks, bands = _consts(half == 0)
        m = dict(shared)
        m.update({"x": xs, "pos": np.ascontiguousarray(ps_.reshape(33, 128).T), "masks": masks, "bands": bands})
        in_maps.append(m)
    if inputs.get("_only_core") is not None:
        c = inputs["_only_core"]
        res = run_bass_kernel_spmd(_NC_CACHE[NT], [in_maps[c]], core_ids=[0])
        return res.results[0]["out"]
    if NT not in _NC_CACHE:
        _NC_CACHE[NT] = build(NT)
    res = run_bass_kernel_spmd(_NC_CACHE[NT], in_maps, core_ids=list(range(8)))
    out = np.zeros((B, SEQ, D), np.float32)
    for core in range(8):
        b, half = core // 2, core % 2
        out[b, half * NTOK:(half + 1) * NTOK] = res.results[core]["out"]
    return out


def kernel(**inputs):
    return _run(inputs, NT_FULL)
```
